# Optimizing a Trainium2 kernel written in Bass

```python
import jax, jax.numpy as jnp
from jax import lax
import numpy as np

D_MODEL = 2048
BATCH = 4
SEQ = 2048
DEPTH = 2
DEC_BATCH = 32
DEC_SEQ = 1
PAST_LEN = 16384
PAGE_SIZE = 128

MIX_WIDTH = D_MODEL
ATTN_WIDTH = MIX_WIDTH // 2
GM_WIDTH = MIX_WIDTH - ATTN_WIDTH
HEAD_DIM = 64
N_HEADS = ATTN_WIDTH // HEAD_DIM
N_KV_HEADS = N_HEADS // 4
GQA_GROUP = N_HEADS // N_KV_HEADS
WINDOW = 128
ROPE_THETA = 10000.0
CHUNK = 128
GM_HEADS = 8
GM_HD = GM_WIDTH // GM_HEADS
D_FF = 11 * D_MODEL // 4
CONV_W = 3
KV_WIDTH = N_KV_HEADS * HEAD_DIM
IN_COLS = ATTN_WIDTH + 2 * KV_WIDTH + 2 * GM_WIDTH
EPS = 1e-6
NEG = -1e30

kernel_name = "hymba_swa_sink_chunk_gmlp_convffn_adaln_step"


def rms_norm(x, g):
    xf = x.astype(jnp.float32)
    y = xf * lax.rsqrt(jnp.mean(xf * xf, axis=-1, keepdims=True) + EPS)
    return (y * g.astype(jnp.float32)).astype(x.dtype)


def rope(x, pos):
    half = HEAD_DIM // 2
    inv = ROPE_THETA ** (-jnp.arange(half, dtype=jnp.float32) / half)
    ang = pos.astype(jnp.float32)[:, None] * inv[None, :]
    cos = jnp.cos(ang)[None, :, None, :]
    sin = jnp.sin(ang)[None, :, None, :]
    xf = x.astype(jnp.float32)
    x1, x2 = xf[..., :half], xf[..., half:]
    return jnp.concatenate([x1 * cos - x2 * sin, x2 * cos + x1 * sin], axis=-1).astype(x.dtype)


def sink_attention(q, k, v, mask, sinks):
    B, N, Tq = q.shape[:3]
    s = jnp.einsum('bnqkgd,bnjkd->bnkgqj', q.astype(jnp.float32), k.astype(jnp.float32)) * (HEAD_DIM ** -0.5)
    s = jnp.where(mask[None, :, None, None], s, NEG)
    sk = sinks.astype(jnp.float32).reshape(N_KV_HEADS, GQA_GROUP)[None, None, :, :, None]
    m = jnp.maximum(s.max(axis=-1), sk)
    p = jnp.exp(s - m[..., None])
    p = p / (p.sum(axis=-1) + jnp.exp(sk - m))[..., None]
    o = jnp.einsum('bnkgqj,bnjkd->bnqkgd', p, v.astype(jnp.float32))
    return o.reshape(B, N * Tq, N_HEADS * HEAD_DIM)


def swa_prompt(q, k, v, sinks):
    B, S = q.shape[:2]
    nb = S // WINDOW
    qb = q.reshape(B, nb, WINDOW, N_KV_HEADS, GQA_GROUP, HEAD_DIM)
    kb = k.reshape(B, nb, WINDOW, N_KV_HEADS, HEAD_DIM)
    vb = v.reshape(B, nb, WINDOW, N_KV_HEADS, HEAD_DIM)
    pad = jnp.zeros_like(kb[:, :1])
    kk = jnp.concatenate([jnp.concatenate([pad, kb[:, :-1]], axis=1), kb], axis=2)
    vv = jnp.concatenate([jnp.concatenate([pad, vb[:, :-1]], axis=1), vb], axis=2)
    n = jnp.arange(nb)[:, None, None]
    qpos = n * WINDOW + jnp.arange(WINDOW)[None, :, None]
    kpos = (n - 1) * WINDOW + jnp.arange(2 * WINDOW)[None, None, :]
    mask = (kpos <= qpos) & (kpos >= qpos - WINDOW) & (kpos >= 0)
    return sink_attention(qb, kk, vv, mask, sinks)


def swa_sample(q, k_new, v_new, k_cache, v_cache, sinks):
    T = q.shape[1]
    W = k_cache.shape[1]
    kk = jnp.concatenate([k_cache.astype(k_new.dtype), k_new], axis=1)
    vv = jnp.concatenate([v_cache.astype(v_new.dtype), v_new], axis=1)
    qpos = PAST_LEN + jnp.arange(T)[:, None]
    kpos = PAST_LEN - W + jnp.arange(W + T)[None, :]
    mask = (kpos <= qpos) & (kpos >= qpos - WINDOW) & (kpos >= 0)
    o = sink_attention(q[:, None], kk[:, None], vv[:, None], mask[None], sinks)
    return o, kk[:, -W:], vv[:, -W:]


def chunk_spatial_gate(u, v, ws, bs):
    B, T, _ = v.shape
    nc = -(-T // CHUNK)
    vp = jnp.pad(v, ((0, 0), (0, nc * CHUNK - T), (0, 0))).reshape(B, nc, CHUNK, GM_HEADS, GM_HD)
    causal = jnp.tril(jnp.ones((CHUNK, CHUNK), dtype=bool))
    wm = jnp.where(causal[None], ws, jnp.zeros_like(ws))
    s = jnp.einsum('hij,bcjhd->bcihd', wm, vp) + jnp.swapaxes(bs, 0, 1)[None, None, :, :, None]
    s = s.reshape(B, nc * CHUNK, GM_WIDTH)[:, :T]
    return u * s


def conv_ffn(h, prev, w_gate, w_up, conv_w, conv_b, w_down):
    a = h @ w_gate
    T = a.shape[1]
    ap = jnp.concatenate([prev.astype(a.dtype), a], axis=1)
    conv = conv_b
    for t in range(CONV_W):
        conv = conv + conv_w[t] * ap[:, t:t + T]
    y = (jax.nn.silu(conv) * (h @ w_up)) @ w_down
    return y, ap[:, -(CONV_W - 1):]


def block(x, c, pos, kv_cache, conv_prev, w_ada, b_ada, g1, g2, w_in, gm_gain, gm_ws, gm_bs,
          sinks, w_out, w_gate, w_up, conv_w, conv_b, w_down):
    B, T, _ = x.shape
    mod = jax.nn.silu(c) @ w_ada + b_ada
    sh1, sc1, gt1, sh2, sc2, gt2 = [m[:, None, :] for m in jnp.split(mod, 6, axis=-1)]
    h = rms_norm(x, g1) * (1 + sc1) + sh1
    z = h @ w_in
    q, k, v, gu, gv = jnp.split(z, [ATTN_WIDTH, ATTN_WIDTH + KV_WIDTH, ATTN_WIDTH + 2 * KV_WIDTH,
                                    ATTN_WIDTH + 2 * KV_WIDTH + GM_WIDTH], axis=-1)
    q = rope(q.reshape(B, T, N_HEADS, HEAD_DIM), pos).reshape(B, T, N_KV_HEADS, GQA_GROUP, HEAD_DIM)
    k = rope(k.reshape(B, T, N_KV_HEADS, HEAD_DIM), pos)
    v = v.reshape(B, T, N_KV_HEADS, HEAD_DIM)
    if kv_cache is None:
        attn = swa_prompt(q, k, v, sinks)
        k_state, v_state = k[:, -WINDOW:], v[:, -WINDOW:]
    else:
        attn, k_state, v_state = swa_sample(q, k, v, kv_cache[0], kv_cache[1], sinks)
    u = jax.nn.gelu(gu)
    gvn = rms_norm(jax.nn.gelu(gv), gm_gain)
    gm = chunk_spatial_gate(u, gvn, gm_ws, gm_bs)
    gm_state = gvn[:, ((T - 1) // CHUNK) * CHUNK:]
    mix = jnp.concatenate([attn.astype(x.dtype), gm], axis=-1) @ w_out
    x = x + gt1 * mix
    h2 = rms_norm(x, g2) * (1 + sc2) + sh2
    f, conv_state = conv_ffn(h2, conv_prev, w_gate, w_up, conv_w, conv_b, w_down)
    x = x + gt2 * f
    return x, k_state, v_state, gm_state, conv_state


def setup_inputs(seed: int = 0) -> dict:
    key = jax.random.key(seed)
    ks = jax.random.split(key, 24)
    nrm = lambda k, shape, s: jax.random.normal(k, shape, dtype=jnp.float32) * s
    D = D_MODEL
    return {
        "x_prompt": nrm(ks[0], (BATCH, SEQ, D), 1.0),
        "x_sample": nrm(ks[1], (DEC_BATCH, DEC_SEQ, D), 1.0),
        "cache_k": nrm(ks[2], (DEPTH, DEC_BATCH, WINDOW, N_KV_HEADS, HEAD_DIM), 1.0),
        "cache_v": nrm(ks[3], (DEPTH, DEC_BATCH, WINDOW, N_KV_HEADS, HEAD_DIM), 1.0),
        "state_conv": nrm(ks[4], (DEPTH, DEC_BATCH, CONV_W - 1, D_FF), 1.0),
        "c_prompt": nrm(ks[5], (BATCH, D), 1.0),
        "c_sample": nrm(ks[6], (DEC_BATCH, D), 1.0),
        "w_ada": nrm(ks[7], (DEPTH, D, 6 * D), 0.3 * D ** -0.5),
        "b_ada": nrm(ks[8], (DEPTH, 6 * D), 0.02),
        "g_norm1": 1.0 + nrm(ks[9], (DEPTH, D), 0.02),
        "g_norm2": 1.0 + nrm(ks[10], (DEPTH, D), 0.02),
        "w_in": nrm(ks[11], (DEPTH, D, IN_COLS), D ** -0.5),
        "gm_gain": 1.0 + nrm(ks[12], (DEPTH, GM_WIDTH), 0.02),
        "gm_ws": nrm(ks[13], (DEPTH, GM_HEADS, CHUNK, CHUNK), CHUNK ** -0.5),
        "gm_bs": 1.0 + nrm(ks[14], (DEPTH, GM_HEADS, CHUNK), 0.02),
        "sinks": nrm(ks[15], (DEPTH, N_HEADS), 1.0),
        "w_out": nrm(ks[16], (DEPTH, MIX_WIDTH, D), MIX_WIDTH ** -0.5),
        "w_gate": nrm(ks[17], (DEPTH, D, D_FF), D ** -0.5),
        "w_up": nrm(ks[18], (DEPTH, D, D_FF), D ** -0.5),
        "conv_w": nrm(ks[19], (DEPTH, CONV_W, D_FF), CONV_W ** -0.5),
        "conv_b": nrm(ks[20], (DEPTH, D_FF), 0.02),
        "w_down": nrm(ks[21], (DEPTH, D_FF, D), D_FF ** -0.5),
        "g_final": 1.0 + nrm(ks[22], (D,), 0.02),
    }


def reference(x_prompt, x_sample, cache_k, cache_v, state_conv, c_prompt, c_sample,
              w_ada, b_ada, g_norm1, g_norm2, w_in, gm_gain, gm_ws, gm_bs, sinks, w_out,
              w_gate, w_up, conv_w, conv_b, w_down, g_final):
    pos_p = jnp.arange(x_prompt.shape[1], dtype=jnp.int32)
    pos_s = PAST_LEN + jnp.arange(x_sample.shape[1], dtype=jnp.int32)
    hp, hs = x_prompt, x_sample
    kp, vp, gp, cp, ksl, vsl, gsl, csl = [], [], [], [], [], [], [], []
    for l in range(DEPTH):
        params = (w_ada[l], b_ada[l], g_norm1[l], g_norm2[l], w_in[l], gm_gain[l], gm_ws[l], gm_bs[l],
                  sinks[l], w_out[l], w_gate[l], w_up[l], conv_w[l], conv_b[l], w_down[l])
        conv0 = jnp.zeros((hp.shape[0], CONV_W - 1, D_FF), dtype=hp.dtype)
        hp, k1, v1, g1, c1 = block(hp, c_prompt, pos_p, None, conv0, *params)
        hs, k2, v2, g2, c2 = block(hs, c_sample, pos_s, (cache_k[l], cache_v[l]), state_conv[l], *params)
        kp.append(k1); vp.append(v1); gp.append(g1); cp.append(c1)
        ksl.append(k2); vsl.append(v2); gsl.append(g2); csl.append(c2)
    y_prompt = rms_norm(hp, g_final)
    y_sample = rms_norm(hs, g_final)
    return (y_prompt, y_sample,
            jnp.stack(kp), jnp.stack(vp), jnp.stack(gp), jnp.stack(cp),
            jnp.stack(ksl), jnp.stack(vsl), jnp.stack(gsl), jnp.stack(csl))
```

```python
import contextlib
import numpy as np
import concourse.bass as bass
import concourse.mybir as mybir
from concourse.bass_utils import run_bass_kernel_spmd

F32 = mybir.dt.float32
BF16 = mybir.dt.bfloat16
AF = mybir.ActivationFunctionType
ALU = mybir.AluOpType

D = 2048
NCH = 16
DFF = 5632
NFF = 44
NBLK = 11
SG = 12
HC = NBLK * 128 + SG
XC = HC - 128
XS = XC - SG
GS = HC - SG
PAST = 16384
EPS = 1e-6
NEG = -30000.0
DEBUG = False


class Sched:
    def __init__(self, nc, es):
        self.nc = nc
        self.engs = {'pe': nc.tensor, 'act': nc.scalar, 'dve': nc.vector, 'pool': nc.gpsimd, 'sp': nc.sync}
        self.semh = {}
        self.cnt = {}
        for k in self.engs:
            self.semh[k] = es.enter_context(nc.semaphore("s_" + k))
            self.cnt[k] = 0
        self.seen = {k: {} for k in self.engs}
        self.regs = {}
        self.dsem = {}
        for q in ('sp', 'pool', 'act'):
            lst = []
            for i in range(6):
                key = "d_%s%d" % (q, i)
                self.semh[key] = es.enter_context(nc.semaphore(key))
                lst.append([key, 0])
            self.dsem[q] = [lst, 0]

    def wait(self, e, tok):
        if tok is None:
            return
        sk, v, src = tok
        if src == 'pe' and e == 'pe':
            return
        if self.seen[e].get(sk, 0) >= v:
            return
        self.engs[e].wait_ge(self.semh[sk], v)
        self.seen[e][sk] = v

    def _deps(self, e, reads, writes):
        for r in reads:
            reg = self.regs.get(r)
            if reg is not None:
                self.wait(e, reg['w'])
        for w in writes:
            reg = self.regs.get(w)
            if reg is not None:
                self.wait(e, reg['w'])
                for t in reg['r']:
                    self.wait(e, t)

    def _upd(self, tok, reads, writes):
        for r in reads:
            reg = self.regs.setdefault(r, {'w': None, 'r': []})
            reg['r'].append(tok)
            if len(reg['r']) > 24:
                reg['r'] = reg['r'][-24:] if False else reg['r']
        for w in writes:
            self.regs[w] = {'w': tok, 'r': []}

    def op(self, e, fn, reads=(), writes=(), inc=True):
        self._deps(e, reads, writes)
        ins = fn(self.engs[e])
        if inc:
            self.cnt[e] += 1
            ins.then_inc(self.semh[e], 1)
            tok = (e, self.cnt[e], e)
        else:
            tok = (e, self.cnt[e] + 1, e)
        self._upd(tok, reads, writes)
        return tok

    def dma(self, q, out, in_, reads=(), writes=(), **kw):
        self._deps(q, reads, writes)
        lst, idx = self.dsem[q]
        ent = lst[idx % len(lst)]
        self.dsem[q][1] = idx + 1
        if ent[1] > 0:
            self.wait(q, (ent[0], 16 * ent[1], 'dma'))
        self.engs[q].dma_start(out=out, in_=in_, **kw).then_inc(self.semh[ent[0]], 16)
        ent[1] += 1
        tok = (ent[0], 16 * ent[1], 'dma')
        self._upd(tok, reads, writes)
        return tok

    def barrier(self):
        for e in self.engs:
            for o in self.engs:
                if o != e and self.cnt[o] > 0:
                    self.wait(e, (o, self.cnt[o], o))
            for q in self.dsem:
                for ent in self.dsem[q][0]:
                    if ent[1] > 0:
                        self.wait(e, (ent[0], 16 * ent[1], 'dma'))

    def finish(self):
        for q in self.dsem:
            for ent in self.dsem[q][0]:
                if ent[1] > 0:
                    self.wait('sp', (ent[0], 16 * ent[1], 'dma'))
        for o in self.engs:
            if o != 'sp' and self.cnt[o] > 0:
                self.wait('sp', (o, self.cnt[o], o))


def col_tiles(g0, g1, maxw=512):
    n = -(-(g1 - g0) // maxw)
    base = (g1 - g0) // n
    rem = (g1 - g0) - base * n
    out = []
    c = g0
    for i in range(n):
        w = base + (1 if i < rem else 0)
        out.append((c, c + w))
        c += w
    return out


def build(dbg_names=()):
    nc = bass.Bass("TRN2", target_bir_lowering=False)
    dram = {}

    def din(name, shape):
        dram[name] = nc.dram_tensor(name, list(shape), F32, kind="ExternalInput").ap()
        return dram[name]

    def dout(name, shape):
        dram[name] = nc.dram_tensor(name, list(shape), F32, kind="ExternalOutput").ap()
        return dram[name]

    xblk = din("xblk", [NBLK, 128, D])
    xs = din("xs", [SG, D])
    cvec = din("cvec", [5, D])
    cache_k = din("cache_k", [2, 4, 128, 256])
    cache_v = din("cache_v", [2, 4, 128, 256])
    state = din("state", [2, 8, DFF])
    w_ada = din("w_ada", [2, D, 6 * D])
    b_ada = din("b_ada", [2, 96, 128])
    g1d = din("g1", [2, 16, 128])
    g2d = din("g2", [2, 16, 128])
    gfd = din("gf", [16, 128])
    w_in = din("w_in", [2, D, 3584])
    gm_gain = din("gm_gain", [2, 1024])
    gm_ws = din("gm_ws", [2, 8, 128, 128])
    gm_bs = din("gm_bs", [2, 1024])
    sinks = din("sinks", [2, 16])
    w_out = din("w_out", [2, D, D])
    w_gate = din("w_gate", [2, D, DFF])
    w_up = din("w_up", [2, D, DFF])
    conv_w = din("conv_w", [2, 132, 128])
    conv_b = din("conv_b", [2, 44, 128])
    w_down = din("w_down", [2, DFF, D])
    rope_c = din("rope_c", [NBLK + 1, 128, 32])
    rope_s = din("rope_s", [NBLK + 1, 128, 32])
    masks = din("masks", [3, 128, 128])
    maskd = din("maskd", [128, 48])
    flagd = din("flag", [128, 1])
    identd = din("ident", [128, 128])

    y_o = dout("y", [8, 128, D])
    ys_o = dout("ys", [SG, D])
    kwin_o = dout("kwin", [2, 128, 256])
    vwin_o = dout("vwin", [2, 128, 256])
    gmv_o = dout("gmv", [2, 128, 1024])
    akeep_o = dout("akeep", [2, 10, DFF])
    kws_o = dout("kws", [2, 4, 128, 256])
    vws_o = dout("vws", [2, 4, 128, 256])
    gmvs_o = dout("gmvs", [2, SG, 1024])

    es = contextlib.ExitStack()
    with es:
        S = Sched(nc, es)

        uniq = [0]

        def sb(name, shape, dt=F32, stack=es):
            uniq[0] += 1
            return stack.enter_context(nc.sbuf_tensor("%s_%d" % (name, uniq[0]), list(shape), dt))

        def ps(name, shape, dt=F32):
            return es.enter_context(nc.psum_tensor(name, list(shape), dt))

        xT = sb("xT", [128, NCH, XC])
        hT = sb("hT", [128, NCH, HC], BF16)
        Wr = [sb("Wr%d" % i, [128, 4096], BF16) for i in range(2)]
        ident_f = sb("ident_f", [128, 128])
        ident_b = sb("ident_b", [128, 128], BF16)
        ones_b = sb("ones_b", [128, 128], BF16)
        mask_b = sb("mask_b", [128, 3, 128], BF16)
        maskD_b = sb("maskD_b", [128, 48], BF16)
        eps_t = sb("eps_t", [128, 1])
        flag_t = sb("flag_t", [128, 1])
        scT = sb("scT", [128, NCH, 5], BF16)
        modT = sb("modT", [128, 2, 96, 5])
        bT = sb("bT", [128, 2, 96])
        gT = sb("gT", [128, 3, 16])
        Ap = sb("Ap", [128, 2, 16])
        As_ = sb("As", [128, 2, 16, 4])
        cwT = sb("cwT", [128, 132])
        cbT = sb("cbT", [128, 44])
        esink = sb("esink", [128, 16])
        gain_b = sb("gain_b", [128, 1024], BF16)
        wmT = sb("wmT", [128, 8, 128], BF16)
        bsrow = sb("bsrow", [1, 8, 128], BF16)
        wsd = sb("wsd", [SG, 8, SG], BF16)
        bs0 = sb("bs0", [1, 8, SG], BF16)
        ropec = sb("ropec", [128, NBLK + 1, 32])
        ropes = sb("ropes", [128, NBLK + 1, 32])
        vt = sb("vt", [128, 264])

        P = [ps("P%d" % i, [128, 512]) for i in range(5)]
        PTb = ps("PTb", [128, 1024], BF16)
        PS1 = ps("PS1", [128, 512])
        PS2 = ps("PS2", [128, 512])

        dbg_outs = {}

        def dbg(name, ap, reads, shape):
            if name not in dbg_names:
                return
            t = dout("dbg_" + name, shape)
            S.dma('sp', t, ap, reads=reads)

        S.dma('sp', ident_f[:], identd, writes=['ident_f'])
        S.op('dve', lambda e: e.tensor_copy(out=ident_b[:], in_=ident_f[:]), reads=['ident_f'], writes=['ident_b'])
        S.op('dve', lambda e: e.memset(ones_b[:], 1.0), writes=['ones_b'])
        S.op('dve', lambda e: e.memset(eps_t[:], EPS), writes=['eps'])
        S.dma('pool', mask_b[:], masks.rearrange("m p n -> p m n"), writes=['mask'])
        S.dma('pool', maskD_b[:], maskd, writes=['mask'])
        S.dma('sp', flag_t[:], flagd, writes=['flag'])
        S.dma('sp', ropec[:], rope_c.rearrange("b p f -> p b f"), writes=['ropec'])
        S.dma('sp', ropes[:], rope_s.rearrange("b p f -> p b f"), writes=['ropes'])

        tp_ctr = [0]

        def load_vecT(src2d, R, dst_ap, dst_key, tmp):
            S.dma('sp', tmp[0:R, 0:128], src2d, writes=['vtmp'])
            S.op('pe', lambda e: e.transpose(out=P[4][:, 0:R], in_=tmp[0:R, 0:128], identity=ident_f[0:R, 0:R]),
                 reads=['vtmp', 'ident_f'], writes=['P4'])
            S.op('dve', lambda e: e.tensor_copy(out=dst_ap, in_=P[4][:, 0:R]), reads=['P4'], writes=[dst_key])

        wctr = [0]
        ring = [[(Wr[0], ('w', 0)), (Wr[1], ('w', 1))]]

        def load_slab(src_ap, nk, ncols):
            slots = ring[0]
            t, key = slots[wctr[0] % len(slots)]
            wctr[0] += 1
            view = t[:, 0:nk * ncols].rearrange("p (k n) -> p k n", k=nk)
            S.dma('pool', view, src_ap.rearrange("(k p) n -> p k n", p=128), writes=[key])
            return view, key

        def ada_load(l, mc, t, off, key, extra_reads=()):
            view = t[:, off:off + 2048].rearrange("p (k n) -> p k n", k=16)
            S.dma('pool', view, w_ada[l, :, mc * 128:(mc + 1) * 128].rearrange("(k p) n -> p k n", p=128),
                  reads=list(extra_reads), writes=[key])
            return view

        def ada_compute(l, mc, view, key):
            pa_, pak = (PS1, 'PS1') if mc % 2 == 0 else (PS2, 'PS2')
            for c in range(NCH):
                S.op('pe', lambda e, c=c: e.matmul(pa_[:, 256:261], view[:, c, :], scT[:, c, :],
                                                   start=(c == 0), stop=(c == NCH - 1)),
                     reads=[key, 'scT'], writes=[pak], inc=(c == NCH - 1))
            S.op('dve', lambda e: e.tensor_scalar(
                out=modT[:, l, mc, :], in0=pa_[:, 256:261], scalar1=bT[:, l, mc:mc + 1], scalar2=None, op0=ALU.add),
                reads=[pak, 'bT'], writes=[('modT', l, mc // 16)])

        class AdaPipe:
            def __init__(self, l, slots, m0=0, m1=96, first_reads=()):
                self.first_reads = first_reads
                self.l, self.slots, self.m, self.views = l, slots, m0, {}
                self.m0, self.m1 = m0, m1
                self.dist = len(slots) - 1

            def step(self):
                if self.m < self.m1:
                    t, off, key = self.slots[self.m % len(self.slots)]
                    self.views[self.m] = (ada_load(self.l, self.m, t, off, key,
                                                   self.first_reads if self.m == self.m0 else ()), key)
                d = self.m - self.dist
                if self.m0 <= d < self.m1:
                    v, key = self.views.pop(d)
                    ada_compute(self.l, d, v, key)
                self.m += 1

            def done(self):
                return self.m >= self.m1 + self.dist

        x0box = [None]

        def load_xblock(b, t, tkey):
            R = 128 if b < NBLK else SG
            S.dma('sp', t[0:R, :], xblk[b] if b < NBLK else xs, writes=[tkey])
            for c4 in range(4):
                pb = P[c4 % 4]
                for j in range(4):
                    c = c4 * 4 + j
                    S.op('pe', lambda e, c=c, j=j, pb=pb: e.transpose(
                        out=pb[:, j * 128:j * 128 + R], in_=t[0:R, c * 128:(c + 1) * 128],
                        identity=ident_f[0:R, 0:R]),
                        reads=[tkey, 'ident_f'], writes=[('P', c4 % 4)], inc=(j == 3))
                src = pb[:, :].rearrange("p (j n) -> p j n", j=4)[:, :, 0:R]
                if b == 0:
                    dst = x0box[0][:, c4 * 4:c4 * 4 + 4, :]
                    wr = [('x0', c) for c in range(c4 * 4, c4 * 4 + 4)]
                else:
                    xc0 = (b - 1) * 128
                    dst = xT[:, c4 * 4:c4 * 4 + 4, xc0:xc0 + R]
                    wr = [('x', c) for c in range(c4 * 4, c4 * 4 + 4)]
                if c4 % 2 == 0:
                    S.op('act', lambda e, dst=dst, src=src: e.activation(out=dst, in_=src, func=AF.Copy),
                         reads=[('P', c4 % 4)], writes=wr)
                else:
                    S.op('dve', lambda e, dst=dst, src=src: e.tensor_copy(out=dst, in_=src),
                         reads=[('P', c4 % 4)], writes=wr)

        with contextlib.ExitStack() as st:
            xtok = [sb("xtok%d" % i, [128, D], F32, st) for i in range(2)]
            S.op('dve', lambda e: e.memset(xT[:, :, XS:XC], 0.0), writes=[('x', c) for c in range(NCH)])
            for b in range(1, NBLK + 1):
                load_xblock(b, xtok[b % 2], ('xtok', b % 2))

            ctok = xtok[0]
            S.dma('sp', ctok[0:5, :], cvec, writes=[('xtok', 0)])
            csb = sb("csb", [5, D], BF16, st)
            S.op('act', lambda e: e.activation(out=csb[:], in_=ctok[0:5, :], func=AF.Silu),
                 reads=[('xtok', 0)], writes=['csb'])
            for c in range(NCH):
                S.op('pe', lambda e, c=c: e.transpose(out=PTb[:, c * 8:c * 8 + 5], in_=csb[0:5, c * 128:(c + 1) * 128],
                                                      identity=ident_b[0:5, 0:5]),
                     reads=['csb', 'ident_b'], writes=['PTb'], inc=(c == NCH - 1))
            S.op('dve', lambda e: e.tensor_copy(out=scT[:], in_=PTb[:, 0:128].rearrange("p (c n) -> p c n", c=NCH)[:, :, 0:5]),
                 reads=['PTb'], writes=['scT'])
            load_vecT(gfd, 16, gT[:, 2, :], 'gT2', vt)
            for l_ in range(2):
                load_vecT(b_ada[l_], 96, bT[:, l_, :], 'bT', vt)
            big = [sb("Wbig%d" % i, [128, 4096], BF16, st) for i in range(3)]
            slots = []
            for i, t in enumerate([Wr[0], Wr[1]] + big):
                slots.append((t, 0, ('wa', i, 0)))
                slots.append((t, 2048, ('wa', i, 1)))
            ap_ = AdaPipe(0, slots, 0, 32, first_reads=[('xtok', 0), ('xtok', 1)])
            while not ap_.done():
                ap_.step()
            S.barrier()

        cur = {}

        def norm(gi, pi, g0, g1, nst, x0src=False):
            shoff = 0 if pi == 0 else 48
            modL = cur['modL']
            mk = ('modT', cur['l'], shoff // 16)
            for (a, bnd) in col_tiles(g0, g1, 336):
                w = bnd - a
                for c in range(NCH):
                    src = x0box[0][:, c, a:bnd] if x0src else xT[:, c, a - 128:bnd - 128]
                    rk = ('x0', c) if x0src else ('x', c)
                    sq = nst['sq'][c % 4]
                    if c % 2 == 0:
                        S.op('act', lambda e, sq=sq, src=src, w=w: e.activation(out=sq[:, 0:w], in_=src, func=AF.Square),
                             reads=[rk], writes=[('sq', c % 4)])
                    else:
                        S.op('dve', lambda e, sq=sq, src=src, w=w: e.tensor_tensor(out=sq[:, 0:w], in0=src, in1=src,
                                                                                   op=ALU.mult),
                             reads=[rk], writes=[('sq', c % 4)])
                    S.op('pe', lambda e, sq=sq, c=c, w=w: e.matmul(P[4][:, 0:w], ones_b[:], sq[:, 0:w],
                                                                   start=(c == 0), stop=(c == NCH - 1)),
                         reads=[('sq', c % 4), 'ones_b'], writes=['P4'])
                rs = nst['rs']
                S.op('act', lambda e, w=w: e.activation(out=rs[:, 0:w], in_=P[4][:, 0:w], func=AF.Sqrt,
                                                        bias=eps_t[:, 0:1], scale=1.0 / D),
                     reads=['P4', 'eps'], writes=['rs'])
                S.op('dve', lambda e, w=w: e.reciprocal(out=rs[:, 0:w], in_=rs[:, 0:w]), reads=['rs'], writes=['rs'])
                has_s = (bnd == HC) and not x0src
                for c in range(NCH):
                    src = x0box[0][:, c, a:bnd] if x0src else xT[:, c, a - 128:bnd - 128]
                    rk = ('x0', c) if x0src else ('x', c)
                    t2 = nst['t2'][c % 2]
                    S.op('dve', lambda e, t2=t2, src=src, w=w, c=c: e.scalar_tensor_tensor(
                        out=t2[:, 0:w], in0=src, scalar=Ap[:, pi, c:c + 1], in1=rs[:, 0:w], op0=ALU.mult, op1=ALU.mult),
                        reads=[rk, 'rs', 'Ap'], writes=[('t2', c % 2)])
                    S.op('act', lambda e, t2=t2, c=c, a=a, bnd=bnd, w=w: e.activation(
                        out=hT[:, c, a:bnd], in_=t2[:, 0:w], func=AF.Identity, bias=modL[:, shoff + c, 0:1], scale=1.0),
                        reads=[('t2', c % 2), mk], writes=[('h', c)])
                    if c % 2 == 1 and cur.get('hook') is not None:
                        cur['hook']()
                    if has_s:
                        o = w - SG + 2
                        t3 = nst['t3']
                        S.op('pool', lambda e, src=src, o=o: e.tensor_tensor(
                            out=t3[:, 0:4], in0=src[:, o:o + 10:3], in1=rs[:, o:o + 10:3], op=ALU.mult),
                            reads=[rk, 'rs'], writes=['t3'])
                        S.op('pool', lambda e, c=c: e.tensor_tensor(
                            out=t3[:, 0:4], in0=t3[:, 0:4], in1=As_[:, pi, c, :], op=ALU.mult),
                            reads=['t3', 'As'], writes=['t3'])
                        S.op('pool', lambda e, c=c: e.tensor_tensor(
                            out=hT[:, c, GS + 2:GS + 12:3], in0=t3[:, 0:4], in1=modL[:, shoff + c, 1:5], op=ALU.add),
                            reads=['t3', mk], writes=[('h', c)])

        for l in range(2):
            cur['modL'] = modT[:, l, :, :]
            cur['l'] = l
            load_vecT(g1d[l], 16, gT[:, 0, :], 'gT0', vt)
            load_vecT(g2d[l], 16, gT[:, 1, :], 'gT1', vt)
            load_vecT(conv_w[l, 0:128, :], 128, cwT[:, 0:128], 'cwT', vt)
            load_vecT(conv_w[l, 128:132, :], 4, cwT[:, 128:132], 'cwT', vt)
            load_vecT(conv_b[l], 44, cbT[:, :], 'cbT', vt)
            S.dma('sp', vt[:, 0:16], sinks[l:l + 1, :].partition_broadcast(128) if False else
                  sinks[l:l + 1, :].to_broadcast([128, 16]), writes=['vtmp'])
            S.op('act', lambda e: e.activation(out=esink[:], in_=vt[:, 0:16], func=AF.Exp),
                 reads=['vtmp'], writes=['esink'])
            S.dma('pool', gain_b[:], gm_gain[l:l + 1, :].to_broadcast([128, 1024]), writes=['gain'])
            S.dma('pool', bsrow[:].rearrange("p h n -> p (h n)"), gm_bs[l:l + 1, :], writes=['bsrow'])
            for h in range(8):
                S.dma('sp', vt[:, 0:128], gm_ws[l, h], writes=['vtmp'])
                S.op('pe', lambda e: e.transpose(out=P[4][:, 0:128], in_=vt[:, 0:128], identity=ident_f[:]),
                     reads=['vtmp', 'ident_f'], writes=['P4'])
                S.op('dve', lambda e, h=h: e.tensor_tensor(out=wmT[:, h, :], in0=P[4][:, 0:128], in1=mask01[:, :],
                                                           op=ALU.mult),
                     reads=['P4', 'mask01'], writes=['wmT']) if False else None
                S.op('dve', lambda e, h=h: e.scalar_tensor_tensor(
                    out=wmT[:, h, :], in0=mask_b[:, 1, 0:128], scalar=1.0 / NEG, in1=P[4][:, 0:128],
                    op0=ALU.mult, op1=ALU.mult), reads=['P4', 'mask'], writes=['wmT']) if False else None
                tmpm = vt[:, 128:256]
                S.op('dve', lambda e: e.tensor_scalar(out=tmpm, in0=mask_b[:, 1, 0:128], scalar1=-1.0 / NEG,
                                                      scalar2=1.0, op0=ALU.mult, op1=ALU.add),
                     reads=['mask'], writes=['vtmp2'])
                S.op('dve', lambda e, h=h: e.tensor_tensor(out=wmT[:, h, :], in0=P[4][:, 0:128], in1=tmpm,
                                                           op=ALU.mult),
                     reads=['P4', 'vtmp2'], writes=['wmT'])
            w00 = vt[0:SG, 256:264]
            S.dma('sp', w00, gm_ws[l, :, 0, 0:1].rearrange("h o -> o h").to_broadcast([SG, 8]), writes=['vtmp3'],
                  allow_slow_non_contiguous=True) if False else None
            for h in range(8):
                S.dma('sp', vt[0:SG, 256 + h:257 + h], gm_ws[l, h, 0:1, 0:1].to_broadcast([SG, 1]),
                      writes=['vtmp3'])
            for h in range(8):
                S.op('dve', lambda e, h=h: e.tensor_scalar(out=wsd[:, h, :], in0=ident_f[0:SG, 0:SG],
                                                           scalar1=vt[0:SG, 256 + h:257 + h], scalar2=None,
                                                           op0=ALU.mult),
                     reads=['vtmp3', 'ident_f'], writes=['wsd'])
                S.op('dve', lambda e, h=h: e.tensor_copy(out=bs0[0:1, h, :], in_=bsrow[0:1, h, 0:1].to_broadcast([1, SG])),
                     reads=['bsrow'], writes=['bs0'])

            def compute_A(pi):
                off = 16 if pi == 0 else 64
                mkk = ('modT', l, off // 16)
                S.op('dve', lambda e: e.scalar_tensor_tensor(
                    out=Ap[:, pi, :], in0=cur['modL'][:, off:off + 16, 0], scalar=1.0, in1=gT[:, pi, :],
                    op0=ALU.add, op1=ALU.mult), reads=[mkk, 'gT0', 'gT1'], writes=['Ap'])
                for k in range(4):
                    S.op('dve', lambda e, k=k: e.scalar_tensor_tensor(
                        out=As_[:, pi, :, k], in0=cur['modL'][:, off:off + 16, 1 + k], scalar=1.0, in1=gT[:, pi, :],
                        op0=ALU.add, op1=ALU.mult), reads=[mkk, 'gT0', 'gT1'], writes=['As'])

            compute_A(0)

            kv0 = 0 if l == 0 else 128
            f0 = 128 if l == 0 else 256
            fb0 = f0 // 128
            n0 = 252 if l == 0 else 382
            fm0 = n0

            with contextlib.ExitStack() as mx:
                t3_ = sb("t3", [128, 4], F32, mx)
                actT = sb("actT", [128, 4, XC], BF16, mx)
                tk = [sb("tk%d" % i, [128, 256], F32, mx) for i in range(4)]
                tkb = sb("tkb", [128, 256], BF16, mx)
                s_ada = contextlib.ExitStack()
                cur['hook'] = None
                adap0 = None
                if l == 0:
                    aflat = actT[:, :, :].rearrange("p k n -> p (k n)")
                    wx1 = sb("wx1", [128, 2048], BF16, s_ada)
                    adap0 = AdaPipe(0, [(aflat, 0, ('wact', 0)), (aflat, 2048, ('wact', 1)), (wx1, 0, ('wact', 2))], 32, 96)

                    def hook0():
                        if not adap0.done():
                            adap0.step()
                    cur['hook'] = hook0
                nst1 = contextlib.ExitStack()
                nst = {'sq': [sb("sq%d" % i, [128, 336], BF16, nst1) for i in range(4)],
                       'rs': sb("rs", [128, 336], F32, nst1),
                       't2': [sb("t2%d" % i, [128, 336], F32, nst1) for i in range(2)],
                       't3': t3_}
                if l == 0:
                    with contextlib.ExitStack() as b0s:
                        xt0 = sb("xt0", [128, D], F32, b0s)
                        x0box[0] = sb("x0T", [128, NCH, 128], F32, b0s)
                        load_xblock(0, xt0, 'xt0')
                        norm(0, 0, 0, 128, nst, x0src=True)
                        S.barrier()
                norm(0, 0, 128, HC, nst)
                S.barrier()
                nst1.close()
                pa = contextlib.ExitStack()
                gvn = sb("gvn", [128, NBLK, 1024], BF16, pa)
                ggf = sb("ggf", [128, 1024], F32, pa)
                ssq = sb("ssq", [128, NBLK, 4], F32, pa)
                rstd = sb("rstd", [128, NBLK], F32, pa)

                rowgroups = [(b, 128) for b in range(fb0, NBLK)] + [(NBLK, SG)]

                def tokmm(Wv, wk, b, R, pbank, pkey, off=0):
                    g0 = b * 128 if b < NBLK else GS
                    for c in range(NCH):
                        S.op('pe', lambda e, c=c: e.matmul(pbank[0:R, off:off + 256], hT[:, c, g0:g0 + R], Wv[:, c, :],
                                                           start=(c == 0), stop=(c == NCH - 1)),
                             reads=[wk, ('h', c)], writes=[pkey], inc=(c == NCH - 1))

                S.op('dve', lambda e: e.memset(ssq[:], 0.0), writes=['ssq'])
                for i in range(4):
                    Wv, wk = load_slab(w_in[l, :, 2560 + i * 256:2560 + (i + 1) * 256], 16, 256)
                    for n, (b, R) in enumerate(rowgroups):
                        slot = b - 1
                        if b == NBLK:
                            pb, pk = P[2 + i // 2], ('P', 2 + i // 2)
                            off = (i % 2) * 256
                            tokmm(Wv, wk, b, R, pb, pk, off)
                            gdst = pb[0:R, off:off + 256]
                            gsrc = gdst
                            gk = pk
                        else:
                            pb, pk = P[n % 2], ('P', n % 2)
                            tokmm(Wv, wk, b, R, pb, pk)
                            gsrc = pb[0:R, 0:256]
                            if b == NBLK - 1:
                                gdst = ggf[0:R, i * 256:(i + 1) * 256]
                                gk = 'ggf'
                            else:
                                gdst = tk[n % 2][0:R, :]
                                gk = ('tk', n % 2)
                        S.op('act', lambda e, gdst=gdst, gsrc=gsrc: e.activation(out=gdst, in_=gsrc, func=AF.Gelu_apprx_tanh),
                             reads=[pk], writes=[gk])
                        if b == NBLK:
                            S.op('act', lambda e, gdst=gdst, R=R, slot=slot, i=i: e.activation(
                                out=gvn[0:R, slot, i * 256:(i + 1) * 256], in_=gdst, func=AF.Square),
                                reads=[gk], writes=[('gvn', slot)])
                        else:
                            S.op('dve', lambda e, gdst=gdst, R=R, slot=slot, i=i: e.tensor_tensor(
                                out=gvn[0:R, slot, i * 256:(i + 1) * 256], in0=gdst, in1=gdst, op=ALU.mult),
                                reads=[gk], writes=[('gvn', slot)])
                        S.op('dve', lambda e, R=R, slot=slot, i=i: e.tensor_reduce(
                            out=ssq[0:R, slot, i:i + 1], in_=gvn[0:R, slot, i * 256:(i + 1) * 256],
                            axis=mybir.AxisListType.X, op=ALU.add),
                            reads=[('gvn', slot)], writes=['ssq'])
                        ce = 'dve' if b == NBLK else 'pool'
                        S.op(ce, lambda e, gdst=gdst, R=R, slot=slot, i=i: e.tensor_copy(
                            out=gvn[0:R, slot, i * 256:(i + 1) * 256], in_=gdst),
                            reads=[gk, 'ssq'], writes=[('gvn', slot)])
                        if cur.get('hook') is not None:
                            cur['hook']()
                if adap0 is not None:
                    while not adap0.done():
                        adap0.step()
                    cur['hook'] = None
                    S.barrier()
                S.op('dve', lambda e: e.tensor_reduce(out=rstd[:, :], in_=ssq[:, :, :], axis=mybir.AxisListType.X,
                                                      op=ALU.add), reads=['ssq'], writes=['rstd'])
                S.op('act', lambda e: e.activation(out=rstd[:, :], in_=rstd[:, :], func=AF.Sqrt, bias=eps_t[:, 0:1],
                                                   scale=1.0 / 1024), reads=['rstd', 'eps'], writes=['rstd'])
                S.op('dve', lambda e: e.reciprocal(out=rstd[:, :], in_=rstd[:, :]), reads=['rstd'], writes=['rstd'])
                for (b, R) in rowgroups:
                    slot = b - 1
                    S.op('dve', lambda e, R=R, slot=slot: e.scalar_tensor_tensor(
                        out=gvn[0:R, slot, :], in0=gvn[0:R, slot, :], scalar=rstd[0:R, slot:slot + 1],
                        in1=gain_b[0:R, :], op0=ALU.mult, op1=ALU.mult),
                        reads=[('gvn', slot), 'rstd', 'gain'], writes=[('gvn', slot)])
                    if b == NBLK - 1:
                        S.op('dve', lambda e, R=R, slot=slot: e.scalar_tensor_tensor(
                            out=ggf[0:R, :], in0=ggf[0:R, :], scalar=rstd[0:R, slot:slot + 1],
                            in1=gain_b[0:R, :], op0=ALU.mult, op1=ALU.mult),
                            reads=['ggf', 'rstd', 'gain'], writes=['ggf'])
                        S.dma('sp', gmv_o[l], ggf[:, :], reads=['ggf'])
                    if b == NBLK:
                        for j in range(4):
                            S.op('dve', lambda e, j=j, slot=slot: e.scalar_tensor_tensor(
                                out=tk[j][0:SG, :], in0=P[2 + j // 2][0:SG, (j % 2) * 256:(j % 2) * 256 + 256],
                                scalar=rstd[0:SG, slot:slot + 1], in1=gain_b[0:SG, j * 256:(j + 1) * 256],
                                op0=ALU.mult, op1=ALU.mult),
                                reads=[('P', 2 + j // 2), 'rstd', 'gain'], writes=[('tk', j)])
                            S.dma('sp', gmvs_o[l][:, j * 256:(j + 1) * 256], tk[j][0:SG, :], reads=[('tk', j)])

                def out_partial(Wsrc, row0, nk, actk, gtoff, g0):
                    tiles = col_tiles(g0, HC)
                    for half in range(2):
                        Wv, wk = load_slab(Wsrc[row0:row0 + nk * 128, half * 1024:(half + 1) * 1024], nk, 1024)
                        for mi in range(8):
                            m = half * 8 + mi
                            for ti, (a, bnd) in enumerate(tiles):
                                w = bnd - a
                                pi_ = (mi * len(tiles) + ti) % 4
                                pb, pk = P[pi_], ('P', pi_)
                                for k in range(nk):
                                    S.op('pe', lambda e, k=k, pb=pb, a=a, bnd=bnd, w=w, mi=mi: e.matmul(
                                        pb[:, 0:w], Wv[:, k, mi * 128:(mi + 1) * 128], actT[:, k, a - 128:bnd - 128],
                                        start=(k == 0), stop=(k == nk - 1)),
                                        reads=[wk, actk], writes=[pk], inc=(k == nk - 1))
                                wp = w - SG if bnd == HC else w
                                S.op('dve', lambda e, pb=pb, m=m, a=a, wp=wp: e.scalar_tensor_tensor(
                                    out=xT[:, m, a - 128:a - 128 + wp], in0=pb[:, 0:wp],
                                    scalar=cur['modL'][:, gtoff + m, 0:1], in1=xT[:, m, a - 128:a - 128 + wp],
                                    op0=ALU.mult, op1=ALU.add), reads=[pk, ('modT', l, 2), ('x', m)], writes=[('x', m)])
                                if bnd == HC:
                                    t3 = nst['t3']
                                    S.op('dve', lambda e, pb=pb, m=m, wp=wp: e.tensor_tensor(
                                        out=t3[:, 0:4], in0=pb[:, wp + 2:wp + 12:3], in1=cur['modL'][:, gtoff + m, 1:5],
                                        op=ALU.mult), reads=[pk, ('modT', l, 2)], writes=['t3'])
                                    S.op('dve', lambda e, m=m: e.tensor_tensor(
                                        out=xT[:, m, XS + 2:XS + 12:3], in0=t3[:, 0:4], in1=xT[:, m, XS + 2:XS + 12:3],
                                        op=ALU.add), reads=['t3', ('x', m)], writes=[('x', m)])

                for i in range(4):
                    Wv, wk = load_slab(w_in[l, :, 1536 + i * 256:1536 + (i + 1) * 256], 16, 256)
                    for j in range(2):
                        h = 2 * i + j
                        kk = h % 4
                        for ti, (a, bnd) in enumerate(col_tiles(fm0, HC)):
                            w = bnd - a
                            pb, pk = P[ti % 2], ('P', ti % 2)
                            for c in range(NCH):
                                S.op('pe', lambda e, c=c, pb=pb, a=a, bnd=bnd, w=w, j=j: e.matmul(
                                    pb[:, 0:w], Wv[:, c, j * 128:(j + 1) * 128], hT[:, c, a:bnd],
                                    start=(c == 0), stop=(c == NCH - 1)),
                                    reads=[wk, ('h', c)], writes=[pk], inc=(c == NCH - 1))
                            S.op('act', lambda e, pb=pb, a=a, bnd=bnd, w=w: e.activation(
                                out=actT[:, kk, a - 128:bnd - 128], in_=pb[:, 0:w], func=AF.Gelu_apprx_tanh),
                                reads=[pk], writes=['actT'])
                        grp = []
                        for (b, R) in rowgroups:
                            grp.append((b, R))
                            if len(grp) == 4 or b == NBLK:
                                pn = 2 + (b % 2)
                                pb, pk = P[pn], ('P', pn)
                                for gi_, (bb, RR) in enumerate(grp):
                                    rw = wmT[:, h, :] if bb < NBLK else wsd[:, h, :]
                                    rb = bsrow[0:1, h, :] if bb < NBLK else bs0[0:1, h, :]
                                    S.op('pe', lambda e, gi_=gi_, bb=bb, RR=RR, rw=rw, pb=pb: e.matmul(
                                        pb[:, gi_ * 128:gi_ * 128 + RR], gvn[0:RR, bb - 1, h * 128:(h + 1) * 128],
                                        rw[0:RR, 0:RR], start=True, stop=False),
                                        reads=[('gvn', bb - 1), 'wmT', 'wsd'], writes=[pk], inc=False)
                                    S.op('pe', lambda e, gi_=gi_, RR=RR, rb=rb, pb=pb: e.matmul(
                                        pb[:, gi_ * 128:gi_ * 128 + RR], ones_b[0:1, :], rb[0:1, 0:RR],
                                        start=False, stop=True),
                                        reads=['bsrow', 'bs0', 'ones_b'], writes=[pk], inc=(gi_ == len(grp) - 1))
                                b0_, R0 = grp[0]
                                xa = (b0_ - 1) * 128
                                for gi_, (bb, RR) in enumerate(grp):
                                    xa2 = (bb - 1) * 128 if bb < NBLK else XS
                                    S.op('dve', lambda e, gi_=gi_, RR=RR, xa2=xa2, pb=pb, kk=kk: e.tensor_tensor(
                                        out=actT[:, kk, xa2:xa2 + RR], in0=pb[:, gi_ * 128:gi_ * 128 + RR],
                                        in1=actT[:, kk, xa2:xa2 + RR], op=ALU.mult),
                                        reads=[pk, 'actT'], writes=['actT'])
                                grp = []
                    if i % 2 == 1:
                        out_partial(w_out[l], 1024 + (i // 2) * 512, 4, 'actT', 32, fm0)

                S.barrier()
                pa.close()
                s_ada.close()
                pbs = contextlib.ExitStack()
                KT = sb("KT", [64, NBLK + 1, 4, 128], BF16, pbs)
                Vg = sb("Vg", [128, NBLK + 1, 4, 65], BF16, pbs)
                cKT = sb("cKT", [64, 4, 128], BF16, pbs)
                cVg = sb("cVg", [128, 4, 65], BF16, pbs)
                tkc = sb("tkc", [128, 256], BF16, pbs)
                S.op('pool', lambda e: e.memset(Vg[:], 1.0), writes=['Vg'])
                S.op('pool', lambda e: e.memset(cVg[:], 1.0), writes=['cVg'])
                kvgroups = [(b, 128) for b in range(kv0 // 128, NBLK)] + [(NBLK, SG)]

                def rope(pb, pk, R, b, dst, dk):
                    src = pb[0:R, 0:256].rearrange("p (h d) -> p h d", h=4)
                    x1, x2 = src[:, :, 0:32], src[:, :, 32:64]
                    cb = ropec[0:R, b:b + 1, :].to_broadcast([R, 4, 32])
                    sn = ropes[0:R, b:b + 1, :].to_broadcast([R, 4, 32])
                    d3 = dst.rearrange("p (h d) -> p h d", h=4)
                    ta = tk[2][0:R, 0:128].rearrange("p (h d) -> p h d", h=4)
                    tb = tk[2][0:R, 128:256].rearrange("p (h d) -> p h d", h=4)
                    S.op('dve', lambda e: e.tensor_tensor(out=ta, in0=x1, in1=cb, op=ALU.mult),
                         reads=[pk, 'ropec'], writes=[('tk', 2)])
                    S.op('dve', lambda e: e.tensor_tensor(out=tb, in0=x2, in1=sn, op=ALU.mult),
                         reads=[pk, 'ropes'], writes=[('tk', 2)])
                    S.op('dve', lambda e: e.tensor_tensor(out=d3[:, :, 0:32], in0=ta, in1=tb, op=ALU.subtract),
                         reads=[('tk', 2)], writes=[dk])
                    S.op('dve', lambda e: e.tensor_tensor(out=ta, in0=x2, in1=cb, op=ALU.mult),
                         reads=[pk, 'ropec', dk], writes=[('tk', 2)])
                    S.op('dve', lambda e: e.tensor_tensor(out=tb, in0=x1, in1=sn, op=ALU.mult),
                         reads=[pk, 'ropes'], writes=[('tk', 2)])
                    S.op('dve', lambda e: e.tensor_tensor(out=d3[:, :, 32:64], in0=ta, in1=tb, op=ALU.add),
                         reads=[('tk', 2)], writes=[dk])

                def tr_heads(srcb, R, dstKT, dkey, skey='tkb'):
                    for hh in range(4):
                        S.op('pe', lambda e, hh=hh: e.transpose(out=PTb[0:64, hh * 128:hh * 128 + R],
                                                                in_=srcb[0:R, hh * 64:(hh + 1) * 64],
                                                                identity=ident_b[0:R, 0:R]),
                             reads=[skey, 'ident_b'], writes=['PTb'], inc=(hh == 3))
                    S.op('act', lambda e: e.activation(
                        out=dstKT, in_=PTb[0:64, 0:512].rearrange("p (h n) -> p h n", h=4)[:, :, 0:R], func=AF.Copy),
                        reads=['PTb'], writes=[dkey])

                Wk_, wkk = load_slab(w_in[l, :, 1024:1280], 16, 256)
                tkbs = [tkb, sb("tkb2", [128, 256], BF16, pbs)]
                tkbk = ['tkb', 'tkb2']

                def kstage1(n):
                    b, R = kvgroups[n]
                    pb, pk = P[n % 2], ('P', n % 2)
                    tokmm(Wk_, wkk, b, R, pb, pk)
                    kd = tk[n % 2]
                    rope(pb, pk, R, b, kd[0:R, :], ('tk', n % 2))
                    if b == NBLK - 1:
                        S.dma('sp', kwin_o[l], kd[:, :], reads=[('tk', n % 2)])
                    if b == NBLK:
                        for k in range(4):
                            S.dma('sp', kws_o[l, k, 127:128, :], kd[3 * k + 2:3 * k + 3, :], reads=[('tk', n % 2)])
                    S.op('pool', lambda e: e.tensor_copy(out=tkbs[n % 2][0:R, :], in_=kd[0:R, :]),
                         reads=[('tk', n % 2)], writes=[tkbk[n % 2]])

                def kstage2(n):
                    b, R = kvgroups[n]
                    tr_heads(tkbs[n % 2], R, KT[:, b, :, 0:R], 'KT', tkbk[n % 2])

                for t in range(len(kvgroups) + 1):
                    if t < len(kvgroups):
                        kstage1(t)
                    if t >= 1:
                        kstage2(t - 1)
                Wv_, wkv = load_slab(w_in[l, :, 1280:1536], 16, 256)
                for n, (b, R) in enumerate(kvgroups):
                    pb, pk = P[n % 2], ('P', n % 2)
                    tokmm(Wv_, wkv, b, R, pb, pk)
                    vd = tk[n % 2]
                    S.op('act', lambda e, vd=vd, pb=pb, R=R: e.activation(out=vd[0:R, :], in_=pb[0:R, 0:256],
                                                                          func=AF.Copy),
                         reads=[pk], writes=[('tk', n % 2)])
                    if b == NBLK - 1:
                        S.dma('sp', vwin_o[l], vd[:, :], reads=[('tk', n % 2)])
                    if b == NBLK:
                        for k in range(4):
                            S.dma('sp', vws_o[l, k, 127:128, :], vd[3 * k + 2:3 * k + 3, :], reads=[('tk', n % 2)])
                    S.op('dve', lambda e, vd=vd, R=R, b=b: e.tensor_copy(
                        out=Vg[0:R, b, :, 0:64], in_=vd[0:R, :].rearrange("p (h d) -> p h d", h=4)),
                        reads=[('tk', n % 2)], writes=['Vg'])
                for k in range(4):
                    S.dma('act', kws_o[l, k, 0:127, :], cache_k[l, k, 1:128, :])
                    S.dma('act', vws_o[l, k, 0:127, :], cache_v[l, k, 1:128, :])

                qTs = [sb("qT", [64, 4, 128], BF16, pbs), sb("qTb", [64, 4, 128], BF16, pbs)]
                PTss = [sb("PTs", [128, 2, 512], BF16, pbs), sb("PTsb", [128, 2, 512], BF16, pbs)]
                den = sb("den", [128, 8], F32, pbs)
                atok = sb("atok", [128, 256], BF16, pbs)
                sacc = tk[2][0:SG, :]

                def att_scores(nq, KTp, KTc, ncur, mP, mC, qb, qk, PTb_, ptk):
                    nn = 4 * nq
                    rq = qb[:, :, 0:nq]
                    S.op('pe', lambda e: e.matmul(PS1[:, 0:nn], KTp, rq, start=True, stop=(mP is None)),
                         reads=[qk, 'KT', 'cKT'], writes=['PS1'], inc=(mP is None))
                    if mP is not None:
                        for g in range(4):
                            S.op('pe', lambda e, g=g: e.matmul(PS1[:, g * 128:(g + 1) * 128], ident_b[:, :], mask_b[:, mP, :],
                                                               start=False, stop=(g == 3)),
                                 reads=['mask', 'ident_b'], writes=['PS1'], inc=(g == 3))
                    S.op('pe', lambda e: e.matmul(PS2[0:ncur, 0:nn], KTc, rq, start=True, stop=False),
                         reads=[qk, 'KT'], writes=['PS2'], inc=False)
                    if mC is None:
                        S.op('pe', lambda e: e.matmul(PS2[0:ncur, 0:nn], ident_b[0:ncur, 0:ncur], maskD_b[0:ncur, 0:nn],
                                                      start=False, stop=True),
                             reads=['mask', 'ident_b'], writes=['PS2'])
                    else:
                        for g in range(4):
                            S.op('pe', lambda e, g=g: e.matmul(PS2[:, g * 128:(g + 1) * 128], ident_b[:, :], mask_b[:, mC, :],
                                                               start=False, stop=(g == 3)),
                                 reads=['mask', 'ident_b'], writes=['PS2'], inc=(g == 3))
                    S.op('act', lambda e: e.activation(out=PTb_[:, 0, 0:nn], in_=PS1[:, 0:nn], func=AF.Exp, scale=0.125),
                         reads=['PS1'], writes=[ptk])
                    S.op('act', lambda e: e.activation(out=PTb_[0:ncur, 1, 0:nn], in_=PS2[0:ncur, 0:nn], func=AF.Exp,
                                                       scale=0.125), reads=['PS2'], writes=[ptk])

                def att_pv(kvh, nq, Vp, Vc, ncur, PTb_, ptk):
                    O = P[4][0:nq, 0:260].rearrange("p (g d) -> p g d", g=4)
                    for g in range(4):
                        S.op('pe', lambda e, g=g: e.matmul(O[:, g, :], PTb_[:, 0, g * nq:(g + 1) * nq], Vp,
                                                           start=True, stop=False),
                             reads=[ptk, 'Vg', 'cVg'], writes=['P4'], inc=False)
                        S.op('pe', lambda e, g=g: e.matmul(O[:, g, :], PTb_[0:ncur, 1, g * nq:(g + 1) * nq], Vc,
                                                           start=False, stop=True),
                             reads=[ptk, 'Vg'], writes=['P4'], inc=(g == 3))
                    S.op('dve', lambda e: e.tensor_tensor(out=den[0:nq, 0:4], in0=O[:, :, 64],
                                                          in1=esink[0:nq, kvh * 4:kvh * 4 + 4], op=ALU.add),
                         reads=['P4', 'esink'], writes=['den'])
                    S.op('dve', lambda e: e.reciprocal(out=den[0:nq, 4:8], in_=den[0:nq, 0:4]),
                         reads=['den'], writes=['den'])
                    return O

                def stageA1(i, n, Wv, wk):
                    b, R = rowgroups[n]
                    pb, pk = P[n % 2], ('P', n % 2)
                    tokmm(Wv, wk, b, R, pb, pk)
                    qd = tk[n % 2]
                    rope(pb, pk, R, b, qd[0:R, :], ('tk', n % 2))
                    S.op('pool', lambda e: e.tensor_copy(out=tkbs[n % 2][0:R, :], in_=qd[0:R, :]),
                         reads=[('tk', n % 2)], writes=[tkbk[n % 2]])

                def stageA2(i, n):
                    b, R = rowgroups[n]
                    tr_heads(tkbs[n % 2], R, qTs[n % 2][:, :, 0:R], ('qT', n % 2), tkbk[n % 2])

                def stageB(i, n):
                    b, R = rowgroups[n]
                    if b < NBLK:
                        att_scores(128, KT[:, b - 1, i, :], KT[:, b, i, :], 128, 2 if b == 3 else 0, 1,
                                   qTs[n % 2], ('qT', n % 2), PTss[n % 2], ('PTs', n % 2))

                def stageC(i, n):
                    b, R = rowgroups[n]
                    PTb_, ptk = PTss[n % 2], ('PTs', n % 2)
                    if b < NBLK:
                        O = att_pv(i, 128, Vg[:, b - 1, i, :], Vg[:, b, i, :], 128, PTb_, ptk)
                        S.op('dve', lambda e: e.tensor_tensor(
                            out=atok[:, :].rearrange("p (g d) -> p g d", g=4), in0=O[:, :, 0:64],
                            in1=den[:, 4:8].unsqueeze(2).to_broadcast([128, 4, 64]), op=ALU.mult),
                            reads=['P4', 'den'], writes=['atok'])
                    else:
                        S.op('dve', lambda e: e.memset(sacc[:], 0.0), writes=[('tk', 2)])
                        for k in range(4):
                            att_scores(SG, cKT[:, k, :], KT[:, NBLK, i, 0:SG], SG, None, None,
                                       qTs[n % 2], ('qT', n % 2), PTb_, ptk)
                            O = att_pv(i, SG, cVg[:, k, :], Vg[0:SG, NBLK, i, :], SG, PTb_, ptk)
                            S.op('dve', lambda e: e.tensor_tensor(
                                out=tk[3][0:SG, :].rearrange("p (g d) -> p g d", g=4), in0=O[:, :, 0:64],
                                in1=den[0:SG, 4:8].unsqueeze(2).to_broadcast([SG, 4, 64]), op=ALU.mult),
                                reads=['P4', 'den'], writes=[('tk', 3)])
                            S.op('dve', lambda e, k=k: e.scalar_tensor_tensor(
                                out=sacc[:, :], in0=tk[3][0:SG, :], scalar=ident_f[0:SG, 3 * k + 2:3 * k + 3],
                                in1=sacc[:, :], op0=ALU.mult, op1=ALU.add),
                                reads=[('tk', 3), 'ident_f', ('tk', 2)], writes=[('tk', 2)])
                        S.op('dve', lambda e: e.tensor_copy(out=atok[0:SG, :], in_=sacc[:, :]),
                             reads=[('tk', 2)], writes=['atok'])
                    xa = (b - 1) * 128 if b < NBLK else XS
                    for j in range(2):
                        S.op('pe', lambda e, j=j: e.transpose(out=PTb[:, 512 + j * 128:512 + j * 128 + R],
                                                              in_=atok[0:R, j * 128:(j + 1) * 128],
                                                              identity=ident_b[0:R, 0:R]),
                             reads=['atok', 'ident_b'], writes=['PTb'], inc=(j == 1))
                    S.op('act', lambda e: e.activation(
                        out=actT[:, 2 * (i % 2):2 * (i % 2) + 2, xa:xa + R],
                        in_=PTb[:, 512:768].rearrange("p (j n) -> p j n", j=2)[:, :, 0:R], func=AF.Copy),
                        reads=['PTb'], writes=['actT'])

                nj = len(rowgroups)
                for i in range(4):
                    Wv, wk = load_slab(w_in[l, :, i * 256:(i + 1) * 256], 16, 256)
                    for k in range(4):
                        S.dma('pool', tkc[:, k * 64:(k + 1) * 64], cache_k[l, k][:, i * 64:(i + 1) * 64], writes=['tkc'])
                        S.dma('pool', cVg[:, k, 0:64], cache_v[l, k][:, i * 64:(i + 1) * 64], writes=['cVg'])
                    tr_heads(tkc, 128, cKT[:, :, :], 'cKT', 'tkc')
                    for t in range(nj + 3):
                        if t < nj:
                            stageA1(i, t, Wv, wk)
                        if 0 <= t - 1 < nj:
                            stageA2(i, t - 1)
                        if 0 <= t - 2 < nj:
                            stageB(i, t - 2)
                        if 0 <= t - 3 < nj:
                            stageC(i, t - 3)
                    if i % 2 == 1:
                        out_partial(w_out[l], (i - 1) * 256, 4, 'actT', 32, fm0)
                S.barrier()
                pbs.close()

            with contextlib.ExitStack() as fx:
                t3_ = sb("t3", [128, 4], F32, fx)
                nst2 = contextlib.ExitStack()
                nst = {'sq': [sb("sq%d" % i, [128, 336], BF16, nst2) for i in range(4)],
                       'rs': sb("rs", [128, 336], F32, nst2),
                       't2': [sb("t2%d" % i, [128, 336], F32, nst2) for i in range(2)],
                       't3': t3_}
                compute_A(1)
                norm(1, 1, n0, HC, nst)
                S.barrier()
                nst2.close()
                W_ = HC - n0
                halo = 2
                aFs = [sb("aF%d" % i, [128, W_ + 2 - halo], F32, fx) for i in range(2)]
                c1 = sb("c1", [128, 512], F32, fx)
                c2 = sb("c2", [128, 512], F32, fx)
                gTt = [sb("gTt%d" % i, [128, 4, XC], BF16, fx) for i in range(2)]
                akeep = sb("akeep", [128, NFF, 10], F32, fx)
                stT = sb("stT", [128, NFF, 8], F32, fx)
                adap = None
                if l == 0:
                    Wx = sb("Wx", [128, 4096], BF16, fx)
                    adap = AdaPipe(1, [(Wx, 0, ('wx', 0)), (Wx, 2048, ('wx', 1))])
                for fc in range(NFF):
                    if fc % 4 == 0:
                        S.dma('sp', c2[0:8, 0:512], state[l, :, fc * 128:fc * 128 + 512], writes=['c2'])
                    S.op('pe', lambda e, fc=fc: e.transpose(out=P[4][:, (fc % 4) * 8:(fc % 4) * 8 + 8],
                                                            in_=c2[0:8, (fc % 4) * 128:(fc % 4 + 1) * 128],
                                                            identity=ident_f[0:8, 0:8]),
                         reads=['c2', 'ident_f'], writes=['P4'], inc=(fc % 4 == 3))
                    if fc % 4 == 3:
                        S.op('dve', lambda e, fc=fc: e.tensor_copy(
                            out=stT[:, fc - 3:fc + 1, :], in_=P[4][:, 0:32].rearrange("p (a b) -> p a b", a=4)),
                            reads=['P4'], writes=['stT'])
                if l == 0:
                    for q2 in range(2):
                        S.op('dve', lambda e, q2=q2: e.memset(aFs[q2][:, 0:2], 0.0), writes=[('aF', q2)])
                tiles = col_tiles(n0, HC)
                so = GS - n0 + 2 - halo
                hidx = 382 - n0 + 2 - halo

                def gate_part(fc, Wg, wkg, aFb, afk):
                    jj = fc % 2
                    gbanks = [(P[0], ('P', 0)), (P[1], ('P', 1))] + ([(PS1, 'PS1')] if l == 1 else [(PTb[:, :].bitcast(F32), 'PTb')])
                    for ti, (a, bnd) in enumerate(tiles):
                        w = bnd - a
                        pb, pk = gbanks[ti % len(gbanks)]
                        for c in range(NCH):
                            S.op('pe', lambda e, c=c: e.matmul(
                                pb[:, 0:w], Wg[:, c, jj * 128:(jj + 1) * 128], hT[:, c, a:bnd],
                                start=(c == 0), stop=(c == NCH - 1)),
                                reads=[wkg, ('h', c)], writes=[pk], inc=(c == NCH - 1))
                        ao = a - n0 + 2 - halo
                        S.op('act', lambda e: e.activation(out=aFb[:, ao:ao + w], in_=pb[:, 0:w], func=AF.Copy),
                             reads=[pk], writes=[afk])
                    S.op('dve', lambda e: e.tensor_scalar(out=aFb[:, hidx:hidx + 2], in0=aFb[:, hidx:hidx + 2],
                                                          scalar1=flag_t[:, 0:1], scalar2=None, op0=ALU.mult),
                         reads=[afk, 'flag'], writes=[afk])
                    S.op('dve', lambda e: e.tensor_copy(
                        out=aFb[:, so:so + 12].rearrange("p (k t) -> p k t", k=4)[:, :, 0:2],
                        in_=stT[:, fc, :].rearrange("p (k t) -> p k t", k=4)),
                        reads=[afk, 'stT'], writes=[afk])
                    S.op('dve', lambda e: e.tensor_copy(out=akeep[:, fc, 0:2], in_=aFb[:, so - 2:so]),
                         reads=[afk], writes=['akeep'])
                    S.op('dve', lambda e: e.tensor_copy(
                        out=akeep[:, fc, 2:10].rearrange("p (k t) -> p k t", k=4),
                        in_=aFb[:, so:so + 12].rearrange("p (k t) -> p k t", k=4)[:, :, 1:3]),
                        reads=[afk], writes=['akeep'])

                def up_part(fc, Wu, wku, aFb, afk):
                    jj = fc % 2
                    j = fc % 4
                    gt_ = gTt[(fc // 4) % 2]
                    gk = ('gTt', (fc // 4) % 2)
                    for ti, (a, bnd) in enumerate(col_tiles(n0 + halo, HC)):
                        w = bnd - a
                        pb, pk = [(P[2], ('P', 2)), (P[3], ('P', 3)), (P[4], 'P4')][ti % 3]
                        for c in range(NCH):
                            S.op('pe', lambda e, c=c: e.matmul(
                                pb[:, 0:w], Wu[:, c, jj * 128:(jj + 1) * 128], hT[:, c, a:bnd],
                                start=(c == 0), stop=(c == NCH - 1)),
                                reads=[wku, ('h', c)], writes=[pk], inc=(c == NCH - 1))
                        ao = a - (n0 + halo)
                        S.op('act', lambda e: e.activation(
                            out=c1[:, 0:w], in_=aFb[:, ao:ao + w], func=AF.Identity,
                            bias=cbT[:, fc:fc + 1], scale=cwT[:, fc:fc + 1]),
                            reads=[afk, 'cwT', 'cbT'], writes=['c1'])
                        S.op('dve', lambda e: e.scalar_tensor_tensor(
                            out=c2[:, 0:w], in0=aFb[:, ao + 1:ao + 1 + w], scalar=cwT[:, 44 + fc:45 + fc],
                            in1=c1[:, 0:w], op0=ALU.mult, op1=ALU.add),
                            reads=[afk, 'cwT', 'c1'], writes=['c2'])
                        S.op('dve', lambda e: e.scalar_tensor_tensor(
                            out=c1[:, 0:w], in0=aFb[:, ao + 2:ao + 2 + w], scalar=cwT[:, 88 + fc:89 + fc],
                            in1=c2[:, 0:w], op0=ALU.mult, op1=ALU.add),
                            reads=[afk, 'cwT', 'c2'], writes=['c1'])
                        S.op('act', lambda e: e.activation(out=c2[:, 0:w], in_=c1[:, 0:w], func=AF.Silu),
                             reads=['c1'], writes=['c2'])
                        S.op('dve', lambda e: e.tensor_tensor(
                            out=gt_[:, j, a - 128:bnd - 128], in0=c2[:, 0:w], in1=pb[:, 0:w], op=ALU.mult),
                            reads=['c2', pk], writes=[gk])

                for fp in range(NFF // 2):
                    Wg, wkg = load_slab(w_gate[l, :, fp * 256:fp * 256 + 256], 16, 256)
                    Wu, wku = load_slab(w_up[l, :, fp * 256:fp * 256 + 256], 16, 256)
                    for q2 in range(2):
                        gate_part(2 * fp + q2, Wg, wkg, aFs[q2], ('aF', q2))
                    for q2 in range(2):
                        fc = 2 * fp + q2
                        up_part(fc, Wu, wku, aFs[q2], ('aF', q2))
                        if adap is not None:
                            want = (98 * (fc + 1)) // NFF
                            while adap.m < want and not adap.done():
                                adap.step()
                    fc = 2 * fp + 1
                    j = fc % 4
                    grp = fc // 4
                    gt_ = gTt[grp % 2]
                    gk = ('gTt', grp % 2)
                    if j == 3:
                        gtiles = col_tiles(n0 + halo, HC)
                        for half in range(2):
                            Wv, wk = load_slab(w_down[l, grp * 512:(grp + 1) * 512, half * 1024:(half + 1) * 1024], 4, 1024)
                            for mi in range(8):
                                m = half * 8 + mi
                                for ti, (a, bnd) in enumerate(gtiles):
                                    w = bnd - a
                                    pi_ = (mi * len(gtiles) + ti) % 4
                                    pb, pk = P[pi_], ('P', pi_)
                                    for k in range(4):
                                        S.op('pe', lambda e, k=k, pb=pb, a=a, bnd=bnd, w=w, mi=mi, Wv=Wv, gt_=gt_: e.matmul(
                                            pb[:, 0:w], Wv[:, k, mi * 128:(mi + 1) * 128], gt_[:, k, a - 128:bnd - 128],
                                            start=(k == 0), stop=(k == 3)),
                                            reads=[wk, gk], writes=[pk], inc=(k == 3))
                                    wp = w - SG if bnd == HC else w
                                    S.op('dve', lambda e, pb=pb, m=m, a=a, wp=wp: e.scalar_tensor_tensor(
                                        out=xT[:, m, a - 128:a - 128 + wp], in0=pb[:, 0:wp],
                                        scalar=cur['modL'][:, 80 + m, 0:1], in1=xT[:, m, a - 128:a - 128 + wp],
                                        op0=ALU.mult, op1=ALU.add), reads=[pk, ('modT', l, 5), ('x', m)], writes=[('x', m)])
                                    if bnd == HC:
                                        t3 = nst['t3']
                                        S.op('dve', lambda e, pb=pb, m=m, wp=wp: e.tensor_tensor(
                                            out=t3[:, 0:4], in0=pb[:, wp + 2:wp + 12:3], in1=cur['modL'][:, 80 + m, 1:5],
                                            op=ALU.mult), reads=[pk, ('modT', l, 5)], writes=['t3'])
                                        S.op('dve', lambda e, m=m: e.tensor_tensor(
                                            out=xT[:, m, XS + 2:XS + 12:3], in0=t3[:, 0:4],
                                            in1=xT[:, m, XS + 2:XS + 12:3], op=ALU.add),
                                            reads=['t3', ('x', m)], writes=[('x', m)])
                for q4 in range(11):
                    for jx in range(4):
                        fc = q4 * 4 + jx
                        S.op('pe', lambda e, fc=fc, jx=jx: e.transpose(out=P[4][0:10, jx * 128:(jx + 1) * 128],
                                                                       in_=akeep[:, fc, :], identity=ident_f[:, :]),
                             reads=['akeep', 'ident_f'], writes=['P4'], inc=(jx == 3))
                    S.op('act', lambda e: e.activation(out=c1[0:10, 0:512], in_=P[4][0:10, 0:512], func=AF.Copy),
                         reads=['P4'], writes=['c1'])
                    S.dma('sp', akeep_o[l, :, q4 * 512:(q4 + 1) * 512], c1[0:10, 0:512], reads=['c1'])
                while adap is not None and not adap.done():
                    adap.step()
                S.barrier()

        with contextlib.ExitStack() as fo:
            sq = [sb("fsq%d" % i, [128, 256], BF16, fo) for i in range(2)]
            rs = sb("frs", [128, 256], F32, fo)
            yt = [sb("yt%d" % i, [128, 128], F32, fo) for i in range(2)]
            yo = [sb("yo%d" % i, [128, D], F32, fo) for i in range(2)]
            groups = [(b, 128) for b in range(3, NBLK)] + [(NBLK, SG)]
            for n, (b, R) in enumerate(groups):
                xa = (b - 1) * 128 if b < NBLK else XS
                for c in range(NCH):
                    S.op('act', lambda e, c=c, xa=xa, R=R: e.activation(out=sq[c % 2][:, 0:R], in_=xT[:, c, xa:xa + R],
                                                                        func=AF.Square),
                         reads=[('x', c)], writes=[('fsq', c % 2)])
                    S.op('pe', lambda e, c=c, R=R: e.matmul(P[4][:, 0:R], ones_b[:], sq[c % 2][:, 0:R],
                                                            start=(c == 0), stop=(c == NCH - 1)),
                         reads=[('fsq', c % 2), 'ones_b'], writes=['P4'])
                S.op('act', lambda e, R=R: e.activation(out=rs[:, 0:R], in_=P[4][:, 0:R], func=AF.Sqrt,
                                                        bias=eps_t[:, 0:1], scale=1.0 / D),
                     reads=['P4', 'eps'], writes=['frs'])
                S.op('dve', lambda e, R=R: e.reciprocal(out=rs[:, 0:R], in_=rs[:, 0:R]), reads=['frs'], writes=['frs'])
                yob = yo[n % 2]
                for c in range(NCH):
                    ytt = yt[c % 2]
                    S.op('dve', lambda e, c=c, xa=xa, R=R, ytt=ytt: e.scalar_tensor_tensor(
                        out=ytt[:, 0:R], in0=xT[:, c, xa:xa + R], scalar=gT[:, 2, c:c + 1], in1=rs[:, 0:R],
                        op0=ALU.mult, op1=ALU.mult), reads=[('x', c), 'gT2', 'frs'], writes=[('yt', c % 2)])
                    pn = c % 4
                    S.op('pe', lambda e, R=R, ytt=ytt, pn=pn: e.transpose(out=P[pn][0:R, 0:128], in_=ytt[:, 0:R],
                                                                          identity=ident_f[:, :]),
                         reads=[('yt', c % 2), 'ident_f'], writes=[('P', pn)])
                    S.op('act', lambda e, c=c, R=R, pn=pn, yob=yob: e.activation(
                        out=yob[0:R, c * 128:(c + 1) * 128], in_=P[pn][0:R, 0:128], func=AF.Copy),
                        reads=[('P', pn)], writes=[('yo', n % 2)])
                if b < NBLK:
                    S.dma('sp', y_o[b - 3], yob[:, :], reads=[('yo', n % 2)])
                else:
                    S.dma('sp', ys_o, yob[0:SG, :], reads=[('yo', n % 2)])
        S.finish()
    return nc


_NC_CACHE = {}


def _rope_tables(pos):
    inv = (np.float32(10000.0) ** (-(np.arange(32, dtype=np.float32) / np.float32(32)))).astype(np.float32)
    ang = (pos.astype(np.float32)[:, None] * inv[None, :]).astype(np.float32)
    return np.cos(ang.astype(np.float64)).astype(np.float32), np.sin(ang.astype(np.float64)).astype(np.float32)


def kernel(**inputs):
    f = lambda k: np.ascontiguousarray(np.asarray(inputs[k], dtype=np.float32))
    x_prompt = f("x_prompt"); x_sample = f("x_sample")
    cache_k = f("cache_k"); cache_v = f("cache_v"); state_conv = f("state_conv")
    c_prompt = f("c_prompt"); c_sample = f("c_sample")
    shared = {
        "w_ada": f("w_ada"), "b_ada": f("b_ada").reshape(2, 96, 128),
        "g1": f("g_norm1").reshape(2, 16, 128), "g2": f("g_norm2").reshape(2, 16, 128),
        "gf": f("g_final").reshape(16, 128), "w_in": f("w_in"), "gm_gain": f("gm_gain"),
        "gm_ws": f("gm_ws"), "gm_bs": f("gm_bs").reshape(2, 1024), "sinks": f("sinks"),
        "w_out": f("w_out"), "w_gate": f("w_gate"), "w_up": f("w_up"),
        "conv_w": f("conv_w").reshape(2, 132, 128), "conv_b": f("conv_b").reshape(2, 44, 128),
        "w_down": f("w_down"), "ident": np.eye(128, dtype=np.float32),
    }
    jj = np.arange(128)[:, None]; ii = np.arange(128)[None, :]
    mP = np.where(jj >= ii, 0.0, NEG).astype(np.float32)
    mC = np.where(jj <= ii, 0.0, NEG).astype(np.float32)
    mD = np.full((128, 48), NEG, np.float32)
    for g in range(4):
        for i in range(SG):
            mD[i, g * SG + i] = 0.0
    in_maps = []
    for ci in range(8):
        bi, half = ci // 2, ci % 2
        s0 = half * 1024
        xb = np.zeros((NBLK, 128, D), np.float32)
        pos = np.zeros((NBLK + 1, 128), np.int64)
        for b in range(NBLK):
            t0 = s0 - 384 + 128 * b
            pos[b] = np.maximum(t0 + np.arange(128), 0)
            if t0 >= 0:
                xb[b] = x_prompt[bi, t0:t0 + 128]
        pos[NBLK] = PAST
        cs, sn = _rope_tables(pos.reshape(-1))
        xs12 = np.zeros((SG, D), np.float32)
        for k in range(4):
            xs12[3 * k + 2] = x_sample[4 * ci + k, 0]
        msk = np.stack([mP, mC, mP if half == 1 else np.full((128, 128), NEG, np.float32)])
        m = dict(shared)
        m.update({
            "xblk": xb, "xs": xs12,
            "cvec": np.concatenate([c_prompt[bi:bi + 1], c_sample[4 * ci:4 * ci + 4]], axis=0),
            "cache_k": np.ascontiguousarray(cache_k[:, 4 * ci:4 * ci + 4].reshape(2, 4, 128, 256)),
            "cache_v": np.ascontiguousarray(cache_v[:, 4 * ci:4 * ci + 4].reshape(2, 4, 128, 256)),
            "state": np.ascontiguousarray(state_conv[:, 4 * ci:4 * ci + 4].reshape(2, 8, DFF)),
            "rope_c": cs.reshape(NBLK + 1, 128, 32), "rope_s": sn.reshape(NBLK + 1, 128, 32),
            "masks": msk.astype(np.float32), "maskd": mD, "flag": np.full((128, 1), float(half), np.float32),
        })
        in_maps.append(m)
    if "nc" not in _NC_CACHE:
        _NC_CACHE["nc"] = build()
    res = run_bass_kernel_spmd(_NC_CACHE["nc"], in_maps, core_ids=list(range(8)))
    R = res.results
    y_prompt = np.zeros((4, 2048, D), np.float32)
    y_sample = np.zeros((32, 1, D), np.float32)
    kwp = np.zeros((2, 4, 128, 4, 64), np.float32); vwp = np.zeros_like(kwp)
    gmp = np.zeros((2, 4, 128, 1024), np.float32)
    cvp = np.zeros((2, 4, 2, DFF), np.float32)
    kws = np.zeros((2, 32, 128, 4, 64), np.float32); vws = np.zeros_like(kws)
    gms = np.zeros((2, 32, 1, 1024), np.float32)
    cvs = np.zeros((2, 32, 2, DFF), np.float32)
    for ci in range(8):
        bi, half = ci // 2, ci % 2
        r = R[ci]
        y_prompt[bi, half * 1024:(half + 1) * 1024] = np.asarray(r["y"]).reshape(1024, D)
        ys = np.asarray(r["ys"]); ak = np.asarray(r["akeep"]); gv = np.asarray(r["gmvs"])
        for k in range(4):
            y_sample[4 * ci + k, 0] = ys[3 * k + 2]
            gms[:, 4 * ci + k, 0] = gv[:, 3 * k + 2]
            cvs[:, 4 * ci + k] = ak[:, 2 + 2 * k:4 + 2 * k]
        kws[:, 4 * ci:4 * ci + 4] = np.asarray(r["kws"]).reshape(2, 4, 128, 4, 64)
        vws[:, 4 * ci:4 * ci + 4] = np.asarray(r["vws"]).reshape(2, 4, 128, 4, 64)
        if half == 1:
            kwp[:, bi] = np.asarray(r["kwin"]).reshape(2, 128, 4, 64)
            vwp[:, bi] = np.asarray(r["vwin"]).reshape(2, 128, 4, 64)
            gmp[:, bi] = np.asarray(r["gmv"])
            cvp[:, bi] = ak[:, 0:2]
    return (y_prompt, y_sample, kwp, vwp, gmp, cvp, kws, vws, gms, cvs)
```

```python
import contextlib
import numpy as np
import concourse.bass as bass
import concourse.mybir as mybir
from concourse.bass_utils import run_bass_kernel_spmd

F32 = mybir.dt.float32
BF16 = mybir.dt.bfloat16
AF = mybir.ActivationFunctionType
ALU = mybir.AluOpType

D = 2048
NCH = 16
DFF = 5632
NFF = 44
NBLK = 11
SG = 12
HC = NBLK * 128 + SG
XC = HC - 128
XS = XC - SG
GS = HC - SG
PAST = 16384
EPS = 1e-6
NEG = -30000.0
DEBUG = False


class Sched:
    def __init__(self, nc, es):
        self.nc = nc
        self.engs = {'pe': nc.tensor, 'act': nc.scalar, 'dve': nc.vector, 'pool': nc.gpsimd, 'sp': nc.sync}
        self.semh = {}
        self.cnt = {}
        for k in self.engs:
            self.semh[k] = es.enter_context(nc.semaphore("s_" + k))
            self.cnt[k] = 0
        self.seen = {k: {} for k in self.engs}
        self.regs = {}
        self.dsem = {}
        for q in ('sp', 'pool', 'act'):
            lst = []
            for i in range(6):
                key = "d_%s%d" % (q, i)
                self.semh[key] = es.enter_context(nc.semaphore(key))
                lst.append([key, 0])
            self.dsem[q] = [lst, 0]

    def wait(self, e, tok):
        if tok is None:
            return
        sk, v, src = tok
        if src == 'pe' and e == 'pe':
            return
        if self.seen[e].get(sk, 0) >= v:
            return
        self.engs[e].wait_ge(self.semh[sk], v)
        self.seen[e][sk] = v

    def _deps(self, e, reads, writes):
        for r in reads:
            reg = self.regs.get(r)
            if reg is not None:
                self.wait(e, reg['w'])
        for w in writes:
            reg = self.regs.get(w)
            if reg is not None:
                self.wait(e, reg['w'])
                for t in reg['r']:
                    self.wait(e, t)

    def _upd(self, tok, reads, writes):
        for r in reads:
            reg = self.regs.setdefault(r, {'w': None, 'r': []})
            reg['r'].append(tok)
            if len(reg['r']) > 24:
                reg['r'] = reg['r'][-24:] if False else reg['r']
        for w in writes:
            self.regs[w] = {'w': tok, 'r': []}

    def op(self, e, fn, reads=(), writes=(), inc=True):
        self._deps(e, reads, writes)
        ins = fn(self.engs[e])
        if inc:
            self.cnt[e] += 1
            ins.then_inc(self.semh[e], 1)
            tok = (e, self.cnt[e], e)
        else:
            tok = (e, self.cnt[e] + 1, e)
        self._upd(tok, reads, writes)
        return tok

    def dma(self, q, out, in_, reads=(), writes=(), **kw):
        self._deps(q, reads, writes)
        lst, idx = self.dsem[q]
        ent = lst[idx % len(lst)]
        self.dsem[q][1] = idx + 1
        if ent[1] > 0:
            self.wait(q, (ent[0], 16 * ent[1], 'dma'))
        self.engs[q].dma_start(out=out, in_=in_, **kw).then_inc(self.semh[ent[0]], 16)
        ent[1] += 1
        tok = (ent[0], 16 * ent[1], 'dma')
        self._upd(tok, reads, writes)
        return tok

    def barrier(self):
        for e in self.engs:
            for o in self.engs:
                if o != e and self.cnt[o] > 0:
                    self.wait(e, (o, self.cnt[o], o))
            for q in self.dsem:
                for ent in self.dsem[q][0]:
                    if ent[1] > 0:
                        self.wait(e, (ent[0], 16 * ent[1], 'dma'))

    def finish(self):
        for q in self.dsem:
            for ent in self.dsem[q][0]:
                if ent[1] > 0:
                    self.wait('sp', (ent[0], 16 * ent[1], 'dma'))
        for o in self.engs:
            if o != 'sp' and self.cnt[o] > 0:
                self.wait('sp', (o, self.cnt[o], o))


def col_tiles(g0, g1, maxw=512):
    n = -(-(g1 - g0) // maxw)
    base = (g1 - g0) // n
    rem = (g1 - g0) - base * n
    out = []
    c = g0
    for i in range(n):
        w = base + (1 if i < rem else 0)
        out.append((c, c + w))
        c += w
    return out


def build(dbg_names=()):
    nc = bass.Bass("TRN2", target_bir_lowering=False)
    dram = {}

    def din(name, shape):
        dram[name] = nc.dram_tensor(name, list(shape), F32, kind="ExternalInput").ap()
        return dram[name]

    def dout(name, shape):
        dram[name] = nc.dram_tensor(name, list(shape), F32, kind="ExternalOutput").ap()
        return dram[name]

    xblk = din("xblk", [NBLK, 128, D])
    xs = din("xs", [SG, D])
    cvec = din("cvec", [5, D])
    cache_k = din("cache_k", [2, 4, 128, 256])
    cache_v = din("cache_v", [2, 4, 128, 256])
    state = din("state", [2, 8, DFF])
    w_ada = din("w_ada", [2, D, 6 * D])
    b_ada = din("b_ada", [2, 96, 128])
    g1d = din("g1", [2, 16, 128])
    g2d = din("g2", [2, 16, 128])
    gfd = din("gf", [16, 128])
    w_in = din("w_in", [2, D, 3584])
    gm_gain = din("gm_gain", [2, 1024])
    gm_ws = din("gm_ws", [2, 8, 128, 128])
    gm_bs = din("gm_bs", [2, 1024])
    sinks = din("sinks", [2, 16])
    w_out = din("w_out", [2, D, D])
    w_gate = din("w_gate", [2, D, DFF])
    w_up = din("w_up", [2, D, DFF])
    conv_w = din("conv_w", [2, 132, 128])
    conv_b = din("conv_b", [2, 44, 128])
    w_down = din("w_down", [2, DFF, D])
    rope_c = din("rope_c", [NBLK + 1, 128, 32])
    rope_s = din("rope_s", [NBLK + 1, 128, 32])
    masks = din("masks", [3, 128, 128])
    maskd = din("maskd", [128, 48])
    flagd = din("flag", [128, 1])
    identd = din("ident", [128, 128])

    y_o = dout("y", [8, 128, D])
    ys_o = dout("ys", [SG, D])
    kwin_o = dout("kwin", [2, 128, 256])
    vwin_o = dout("vwin", [2, 128, 256])
    gmv_o = dout("gmv", [2, 128, 1024])
    akeep_o = dout("akeep", [2, 10, DFF])
    kws_o = dout("kws", [2, 4, 128, 256])
    vws_o = dout("vws", [2, 4, 128, 256])
    gmvs_o = dout("gmvs", [2, SG, 1024])

    es = contextlib.ExitStack()
    with es:
        S = Sched(nc, es)

        uniq = [0]

        def sb(name, shape, dt=F32, stack=es):
            uniq[0] += 1
            return stack.enter_context(nc.sbuf_tensor("%s_%d" % (name, uniq[0]), list(shape), dt))

        def ps(name, shape, dt=F32):
            return es.enter_context(nc.psum_tensor(name, list(shape), dt))

        xT = sb("xT", [128, NCH, XC])
        hT = sb("hT", [128, NCH, HC], BF16)
        Wr = [sb("Wr%d" % i, [128, 4096], BF16) for i in range(2)]
        ident_f = sb("ident_f", [128, 128])
        ident_b = sb("ident_b", [128, 128], BF16)
        ones_b = sb("ones_b", [128, 128], BF16)
        mask_b = sb("mask_b", [128, 3, 128], BF16)
        maskD_b = sb("maskD_b", [128, 48], BF16)
        eps_t = sb("eps_t", [128, 1])
        flag_t = sb("flag_t", [128, 1])
        scT = sb("scT", [128, NCH, 5], BF16)
        modT = sb("modT", [128, 2, 96, 5])
        bT = sb("bT", [128, 2, 96])
        gT = sb("gT", [128, 3, 16])
        Ap = sb("Ap", [128, 2, 16])
        As_ = sb("As", [128, 2, 16, 4])
        cwT = sb("cwT", [128, 132])
        cbT = sb("cbT", [128, 44])
        esink = sb("esink", [128, 16])
        gain_b = sb("gain_b", [128, 1024], BF16)
        wmT = sb("wmT", [128, 8, 128], BF16)
        bsrow = sb("bsrow", [1, 8, 128], BF16)
        wsd = sb("wsd", [SG, 8, SG], BF16)
        bs0 = sb("bs0", [1, 8, SG], BF16)
        ropec = sb("ropec", [128, NBLK + 1, 32])
        ropes = sb("ropes", [128, NBLK + 1, 32])
        vt = sb("vt", [128, 264])

        P = [ps("P%d" % i, [128, 512]) for i in range(5)]
        PTb = ps("PTb", [128, 1024], BF16)
        PS1 = ps("PS1", [128, 512])
        PS2 = ps("PS2", [128, 512])

        dbg_outs = {}

        def dbg(name, ap, reads, shape):
            if name not in dbg_names:
                return
            t = dout("dbg_" + name, shape)
            S.dma('sp', t, ap, reads=reads)

        S.dma('sp', ident_f[:], identd, writes=['ident_f'])
        S.op('dve', lambda e: e.tensor_copy(out=ident_b[:], in_=ident_f[:]), reads=['ident_f'], writes=['ident_b'])
        S.op('dve', lambda e: e.memset(ones_b[:], 1.0), writes=['ones_b'])
        S.op('dve', lambda e: e.memset(eps_t[:], EPS), writes=['eps'])
        S.dma('pool', mask_b[:], masks.rearrange("m p n -> p m n"), writes=['mask'])
        S.dma('pool', maskD_b[:], maskd, writes=['mask'])
        S.dma('sp', flag_t[:], flagd, writes=['flag'])
        S.dma('sp', ropec[:], rope_c.rearrange("b p f -> p b f"), writes=['ropec'])
        S.dma('sp', ropes[:], rope_s.rearrange("b p f -> p b f"), writes=['ropes'])

        tp_ctr = [0]

        def load_vecT(src2d, R, dst_ap, dst_key, tmp):
            S.dma('sp', tmp[0:R, 0:128], src2d, writes=['vtmp'])
            S.op('pe', lambda e: e.transpose(out=P[4][:, 0:R], in_=tmp[0:R, 0:128], identity=ident_f[0:R, 0:R]),
                 reads=['vtmp', 'ident_f'], writes=['P4'])
            S.op('dve', lambda e: e.tensor_copy(out=dst_ap, in_=P[4][:, 0:R]), reads=['P4'], writes=[dst_key])

        wctr = [0]
        ring = [[(Wr[0], ('w', 0)), (Wr[1], ('w', 1))]]

        def load_slab(src_ap, nk, ncols):
            slots = ring[0]
            t, key = slots[wctr[0] % len(slots)]
            wctr[0] += 1
            view = t[:, 0:nk * ncols].rearrange("p (k n) -> p k n", k=nk)
            S.dma('pool', view, src_ap.rearrange("(k p) n -> p k n", p=128), writes=[key])
            return view, key

        def ada_load(l, mc, t, off, key, extra_reads=()):
            view = t[:, off:off + 2048].rearrange("p (k n) -> p k n", k=16)
            S.dma('pool', view, w_ada[l, :, mc * 128:(mc + 1) * 128].rearrange("(k p) n -> p k n", p=128),
                  reads=list(extra_reads), writes=[key])
            return view

        def ada_compute(l, mc, view, key):
            pa_, pak = (PS1, 'PS1') if mc % 2 == 0 else (PS2, 'PS2')
            for c in range(NCH):
                S.op('pe', lambda e, c=c: e.matmul(pa_[:, 256:261], view[:, c, :], scT[:, c, :],
                                                   start=(c == 0), stop=(c == NCH - 1)),
                     reads=[key, 'scT'], writes=[pak], inc=(c == NCH - 1))
            S.op('dve', lambda e: e.tensor_scalar(
                out=modT[:, l, mc, :], in0=pa_[:, 256:261], scalar1=bT[:, l, mc:mc + 1], scalar2=None, op0=ALU.add),
                reads=[pak, 'bT'], writes=[('modT', l, mc // 16)])

        class AdaPipe:
            def __init__(self, l, slots, m0=0, m1=96, first_reads=()):
                self.first_reads = first_reads
                self.l, self.slots, self.m, self.views = l, slots, m0, {}
                self.m0, self.m1 = m0, m1
                self.dist = len(slots) - 1

            def step(self):
                if self.m < self.m1:
                    t, off, key = self.slots[self.m % len(self.slots)]
                    self.views[self.m] = (ada_load(self.l, self.m, t, off, key,
                                                   self.first_reads if self.m == self.m0 else ()), key)
                d = self.m - self.dist
                if self.m0 <= d < self.m1:
                    v, key = self.views.pop(d)
                    ada_compute(self.l, d, v, key)
                self.m += 1

            def done(self):
                return self.m >= self.m1 + self.dist

        x0box = [None]

        def load_xblock(b, t, tkey):
            R = 128 if b < NBLK else SG
            S.dma('sp', t[0:R, :], xblk[b] if b < NBLK else xs, writes=[tkey])
            for c4 in range(4):
                pb = P[c4 % 4]
                for j in range(4):
                    c = c4 * 4 + j
                    S.op('pe', lambda e, c=c, j=j, pb=pb: e.transpose(
                        out=pb[:, j * 128:j * 128 + R], in_=t[0:R, c * 128:(c + 1) * 128],
                        identity=ident_f[0:R, 0:R]),
                        reads=[tkey, 'ident_f'], writes=[('P', c4 % 4)], inc=(j == 3))
                src = pb[:, :].rearrange("p (j n) -> p j n", j=4)[:, :, 0:R]
                if b == 0:
                    dst = x0box[0][:, c4 * 4:c4 * 4 + 4, :]
                    wr = [('x0', c) for c in range(c4 * 4, c4 * 4 + 4)]
                else:
                    xc0 = (b - 1) * 128
                    dst = xT[:, c4 * 4:c4 * 4 + 4, xc0:xc0 + R]
                    wr = [('x', c) for c in range(c4 * 4, c4 * 4 + 4)]
                if c4 % 2 == 0:
                    S.op('act', lambda e, dst=dst, src=src: e.activation(out=dst, in_=src, func=AF.Copy),
                         reads=[('P', c4 % 4)], writes=wr)
                else:
                    S.op('dve', lambda e, dst=dst, src=src: e.tensor_copy(out=dst, in_=src),
                         reads=[('P', c4 % 4)], writes=wr)

        with contextlib.ExitStack() as st:
            xtok = [sb("xtok%d" % i, [128, D], F32, st) for i in range(2)]
            S.op('dve', lambda e: e.memset(xT[:, :, XS:XC], 0.0), writes=[('x', c) for c in range(NCH)])
            for b in range(1, NBLK + 1):
                load_xblock(b, xtok[b % 2], ('xtok', b % 2))

            ctok = xtok[0]
            S.dma('sp', ctok[0:5, :], cvec, writes=[('xtok', 0)])
            csb = sb("csb", [5, D], BF16, st)
            S.op('act', lambda e: e.activation(out=csb[:], in_=ctok[0:5, :], func=AF.Silu),
                 reads=[('xtok', 0)], writes=['csb'])
            for c in range(NCH):
                S.op('pe', lambda e, c=c: e.transpose(out=PTb[:, c * 8:c * 8 + 5], in_=csb[0:5, c * 128:(c + 1) * 128],
                                                      identity=ident_b[0:5, 0:5]),
                     reads=['csb', 'ident_b'], writes=['PTb'], inc=(c == NCH - 1))
            S.op('dve', lambda e: e.tensor_copy(out=scT[:], in_=PTb[:, 0:128].rearrange("p (c n) -> p c n", c=NCH)[:, :, 0:5]),
                 reads=['PTb'], writes=['scT'])
            load_vecT(gfd, 16, gT[:, 2, :], 'gT2', vt)
            for l_ in range(2):
                load_vecT(b_ada[l_], 96, bT[:, l_, :], 'bT', vt)
            big = [sb("Wbig%d" % i, [128, 4096], BF16, st) for i in range(3)]
            slots = []
            for i, t in enumerate([Wr[0], Wr[1]] + big):
                slots.append((t, 0, ('wa', i, 0)))
                slots.append((t, 2048, ('wa', i, 1)))
            ap_ = AdaPipe(0, slots, 0, 32, first_reads=[('xtok', 0), ('xtok', 1)])
            while not ap_.done():
                ap_.step()
            S.barrier()

        cur = {}

        def norm(gi, pi, g0, g1, nst, x0src=False):
            shoff = 0 if pi == 0 else 48
            modL = cur['modL']
            mk = ('modT', cur['l'], shoff // 16)
            for (a, bnd) in col_tiles(g0, g1, 336):
                w = bnd - a
                for c in range(NCH):
                    src = x0box[0][:, c, a:bnd] if x0src else xT[:, c, a - 128:bnd - 128]
                    rk = ('x0', c) if x0src else ('x', c)
                    sq = nst['sq'][c % 4]
                    if c % 2 == 0:
                        S.op('act', lambda e, sq=sq, src=src, w=w: e.activation(out=sq[:, 0:w], in_=src, func=AF.Square),
                             reads=[rk], writes=[('sq', c % 4)])
                    else:
                        S.op('dve', lambda e, sq=sq, src=src, w=w: e.tensor_tensor(out=sq[:, 0:w], in0=src, in1=src,
                                                                                   op=ALU.mult),
                             reads=[rk], writes=[('sq', c % 4)])
                    S.op('pe', lambda e, sq=sq, c=c, w=w: e.matmul(P[4][:, 0:w], ones_b[:], sq[:, 0:w],
                                                                   start=(c == 0), stop=(c == NCH - 1)),
                         reads=[('sq', c % 4), 'ones_b'], writes=['P4'])
                rs = nst['rs']
                S.op('act', lambda e, w=w: e.activation(out=rs[:, 0:w], in_=P[4][:, 0:w], func=AF.Sqrt,
                                                        bias=eps_t[:, 0:1], scale=1.0 / D),
                     reads=['P4', 'eps'], writes=['rs'])
                S.op('dve', lambda e, w=w: e.reciprocal(out=rs[:, 0:w], in_=rs[:, 0:w]), reads=['rs'], writes=['rs'])
                has_s = (bnd == HC) and not x0src
                for c in range(NCH):
                    src = x0box[0][:, c, a:bnd] if x0src else xT[:, c, a - 128:bnd - 128]
                    rk = ('x0', c) if x0src else ('x', c)
                    t2 = nst['t2'][c % 2]
                    S.op('dve', lambda e, t2=t2, src=src, w=w, c=c: e.scalar_tensor_tensor(
                        out=t2[:, 0:w], in0=src, scalar=Ap[:, pi, c:c + 1], in1=rs[:, 0:w], op0=ALU.mult, op1=ALU.mult),
                        reads=[rk, 'rs', 'Ap'], writes=[('t2', c % 2)])
                    S.op('act', lambda e, t2=t2, c=c, a=a, bnd=bnd, w=w: e.activation(
                        out=hT[:, c, a:bnd], in_=t2[:, 0:w], func=AF.Identity, bias=modL[:, shoff + c, 0:1], scale=1.0),
                        reads=[('t2', c % 2), mk], writes=[('h', c)])
                    if c % 2 == 1 and cur.get('hook') is not None:
                        cur['hook']()
                    if has_s:
                        o = w - SG + 2
                        t3 = nst['t3']
                        S.op('pool', lambda e, src=src, o=o: e.tensor_tensor(
                            out=t3[:, 0:4], in0=src[:, o:o + 10:3], in1=rs[:, o:o + 10:3], op=ALU.mult),
                            reads=[rk, 'rs'], writes=['t3'])
                        S.op('pool', lambda e, c=c: e.tensor_tensor(
                            out=t3[:, 0:4], in0=t3[:, 0:4], in1=As_[:, pi, c, :], op=ALU.mult),
                            reads=['t3', 'As'], writes=['t3'])
                        S.op('pool', lambda e, c=c: e.tensor_tensor(
                            out=hT[:, c, GS + 2:GS + 12:3], in0=t3[:, 0:4], in1=modL[:, shoff + c, 1:5], op=ALU.add),
                            reads=['t3', mk], writes=[('h', c)])

        for l in range(2):
            cur['modL'] = modT[:, l, :, :]
            cur['l'] = l
            load_vecT(g1d[l], 16, gT[:, 0, :], 'gT0', vt)
            load_vecT(g2d[l], 16, gT[:, 1, :], 'gT1', vt)
            load_vecT(conv_w[l, 0:128, :], 128, cwT[:, 0:128], 'cwT', vt)
            load_vecT(conv_w[l, 128:132, :], 4, cwT[:, 128:132], 'cwT', vt)
            load_vecT(conv_b[l], 44, cbT[:, :], 'cbT', vt)
            S.dma('sp', vt[:, 0:16], sinks[l:l + 1, :].partition_broadcast(128) if False else
                  sinks[l:l + 1, :].to_broadcast([128, 16]), writes=['vtmp'])
            S.op('act', lambda e: e.activation(out=esink[:], in_=vt[:, 0:16], func=AF.Exp),
                 reads=['vtmp'], writes=['esink'])
            S.dma('pool', gain_b[:], gm_gain[l:l + 1, :].to_broadcast([128, 1024]), writes=['gain'])
            S.dma('pool', bsrow[:].rearrange("p h n -> p (h n)"), gm_bs[l:l + 1, :], writes=['bsrow'])
            for h in range(8):
                S.dma('sp', vt[:, 0:128], gm_ws[l, h], writes=['vtmp'])
                S.op('pe', lambda e: e.transpose(out=P[4][:, 0:128], in_=vt[:, 0:128], identity=ident_f[:]),
                     reads=['vtmp', 'ident_f'], writes=['P4'])
                S.op('dve', lambda e, h=h: e.tensor_tensor(out=wmT[:, h, :], in0=P[4][:, 0:128], in1=mask01[:, :],
                                                           op=ALU.mult),
                     reads=['P4', 'mask01'], writes=['wmT']) if False else None
                S.op('dve', lambda e, h=h: e.scalar_tensor_tensor(
                    out=wmT[:, h, :], in0=mask_b[:, 1, 0:128], scalar=1.0 / NEG, in1=P[4][:, 0:128],
                    op0=ALU.mult, op1=ALU.mult), reads=['P4', 'mask'], writes=['wmT']) if False else None
                tmpm = vt[:, 128:256]
                S.op('dve', lambda e: e.tensor_scalar(out=tmpm, in0=mask_b[:, 1, 0:128], scalar1=-1.0 / NEG,
                                                      scalar2=1.0, op0=ALU.mult, op1=ALU.add),
                     reads=['mask'], writes=['vtmp2'])
                S.op('dve', lambda e, h=h: e.tensor_tensor(out=wmT[:, h, :], in0=P[4][:, 0:128], in1=tmpm,
                                                           op=ALU.mult),
                     reads=['P4', 'vtmp2'], writes=['wmT'])
            w00 = vt[0:SG, 256:264]
            S.dma('sp', w00, gm_ws[l, :, 0, 0:1].rearrange("h o -> o h").to_broadcast([SG, 8]), writes=['vtmp3'],
                  allow_slow_non_contiguous=True) if False else None
            for h in range(8):
                S.dma('sp', vt[0:SG, 256 + h:257 + h], gm_ws[l, h, 0:1, 0:1].to_broadcast([SG, 1]),
                      writes=['vtmp3'])
            for h in range(8):
                S.op('dve', lambda e, h=h: e.tensor_scalar(out=wsd[:, h, :], in0=ident_f[0:SG, 0:SG],
                                                           scalar1=vt[0:SG, 256 + h:257 + h], scalar2=None,
                                                           op0=ALU.mult),
                     reads=['vtmp3', 'ident_f'], writes=['wsd'])
                S.op('dve', lambda e, h=h: e.tensor_copy(out=bs0[0:1, h, :], in_=bsrow[0:1, h, 0:1].to_broadcast([1, SG])),
                     reads=['bsrow'], writes=['bs0'])

            def compute_A(pi):
                off = 16 if pi == 0 else 64
                mkk = ('modT', l, off // 16)
                S.op('dve', lambda e: e.scalar_tensor_tensor(
                    out=Ap[:, pi, :], in0=cur['modL'][:, off:off + 16, 0], scalar=1.0, in1=gT[:, pi, :],
                    op0=ALU.add, op1=ALU.mult), reads=[mkk, 'gT0', 'gT1'], writes=['Ap'])
                for k in range(4):
                    S.op('dve', lambda e, k=k: e.scalar_tensor_tensor(
                        out=As_[:, pi, :, k], in0=cur['modL'][:, off:off + 16, 1 + k], scalar=1.0, in1=gT[:, pi, :],
                        op0=ALU.add, op1=ALU.mult), reads=[mkk, 'gT0', 'gT1'], writes=['As'])

            compute_A(0)

            kv0 = 0 if l == 0 else 128
            f0 = 128 if l == 0 else 256
            fb0 = f0 // 128
            n0 = 252 if l == 0 else 382
            fm0 = n0

            with contextlib.ExitStack() as mx:
                t3_ = sb("t3", [128, 4], F32, mx)
                actT = sb("actT", [128, 4, XC], BF16, mx)
                tk = [sb("tk%d" % i, [128, 256], F32, mx) for i in range(4)]
                tkb = sb("tkb", [128, 256], BF16, mx)
                s_ada = contextlib.ExitStack()
                cur['hook'] = None
                adap0 = None
                if l == 0:
                    aflat = actT[:, :, :].rearrange("p k n -> p (k n)")
                    wx1 = sb("wx1", [128, 2048], BF16, s_ada)
                    adap0 = AdaPipe(0, [(aflat, 0, ('wact', 0)), (aflat, 2048, ('wact', 1)), (wx1, 0, ('wact', 2))], 32, 96)

                    def hook0():
                        if not adap0.done():
                            adap0.step()
                    cur['hook'] = hook0
                nst1 = contextlib.ExitStack()
                nst = {'sq': [sb("sq%d" % i, [128, 336], BF16, nst1) for i in range(4)],
                       'rs': sb("rs", [128, 336], F32, nst1),
                       't2': [sb("t2%d" % i, [128, 336], F32, nst1) for i in range(2)],
                       't3': t3_}
                if l == 0:
                    with contextlib.ExitStack() as b0s:
                        xt0 = sb("xt0", [128, D], F32, b0s)
                        x0box[0] = sb("x0T", [128, NCH, 128], F32, b0s)
                        load_xblock(0, xt0, 'xt0')
                        norm(0, 0, 0, 128, nst, x0src=True)
                        S.barrier()
                norm(0, 0, 128, HC, nst)
                S.barrier()
                nst1.close()
                pa = contextlib.ExitStack()
                gvn = sb("gvn", [128, NBLK, 1024], BF16, pa)
                ggf = sb("ggf", [128, 1024], F32, pa)
                ssq = sb("ssq", [128, NBLK, 4], F32, pa)
                rstd = sb("rstd", [128, NBLK], F32, pa)

                rowgroups = [(b, 128) for b in range(fb0, NBLK)] + [(NBLK, SG)]

                def tokmm(Wv, wk, b, R, pbank, pkey, off=0):
                    g0 = b * 128 if b < NBLK else GS
                    for c in range(NCH):
                        S.op('pe', lambda e, c=c: e.matmul(pbank[0:R, off:off + 256], hT[:, c, g0:g0 + R], Wv[:, c, :],
                                                           start=(c == 0), stop=(c == NCH - 1)),
                             reads=[wk, ('h', c)], writes=[pkey], inc=(c == NCH - 1))

                S.op('dve', lambda e: e.memset(ssq[:], 0.0), writes=['ssq'])
                for i in range(4):
                    Wv, wk = load_slab(w_in[l, :, 2560 + i * 256:2560 + (i + 1) * 256], 16, 256)
                    for n, (b, R) in enumerate(rowgroups):
                        slot = b - 1
                        if b == NBLK:
                            pb, pk = P[2 + i // 2], ('P', 2 + i // 2)
                            off = (i % 2) * 256
                            tokmm(Wv, wk, b, R, pb, pk, off)
                            gdst = pb[0:R, off:off + 256]
                            gsrc = gdst
                            gk = pk
                        else:
                            pb, pk = P[n % 2], ('P', n % 2)
                            tokmm(Wv, wk, b, R, pb, pk)
                            gsrc = pb[0:R, 0:256]
                            if b == NBLK - 1:
                                gdst = ggf[0:R, i * 256:(i + 1) * 256]
                                gk = 'ggf'
                            else:
                                gdst = tk[n % 2][0:R, :]
                                gk = ('tk', n % 2)
                        S.op('act', lambda e, gdst=gdst, gsrc=gsrc: e.activation(out=gdst, in_=gsrc, func=AF.Gelu_apprx_tanh),
                             reads=[pk], writes=[gk])
                        if b == NBLK:
                            S.op('act', lambda e, gdst=gdst, R=R, slot=slot, i=i: e.activation(
                                out=gvn[0:R, slot, i * 256:(i + 1) * 256], in_=gdst, func=AF.Square),
                                reads=[gk], writes=[('gvn', slot)])
                        else:
                            S.op('dve', lambda e, gdst=gdst, R=R, slot=slot, i=i: e.tensor_tensor(
                                out=gvn[0:R, slot, i * 256:(i + 1) * 256], in0=gdst, in1=gdst, op=ALU.mult),
                                reads=[gk], writes=[('gvn', slot)])
                        S.op('dve', lambda e, R=R, slot=slot, i=i: e.tensor_reduce(
                            out=ssq[0:R, slot, i:i + 1], in_=gvn[0:R, slot, i * 256:(i + 1) * 256],
                            axis=mybir.AxisListType.X, op=ALU.add),
                            reads=[('gvn', slot)], writes=['ssq'])
                        ce = 'dve' if b == NBLK else 'pool'
                        S.op(ce, lambda e, gdst=gdst, R=R, slot=slot, i=i: e.tensor_copy(
                            out=gvn[0:R, slot, i * 256:(i + 1) * 256], in_=gdst),
                            reads=[gk, 'ssq'], writes=[('gvn', slot)])
                        if cur.get('hook') is not None:
                            cur['hook']()
                if adap0 is not None:
                    while not adap0.done():
                        adap0.step()
                    cur['hook'] = None
                    S.barrier()
                S.op('dve', lambda e: e.tensor_reduce(out=rstd[:, :], in_=ssq[:, :, :], axis=mybir.AxisListType.X,
                                                      op=ALU.add), reads=['ssq'], writes=['rstd'])
                S.op('act', lambda e: e.activation(out=rstd[:, :], in_=rstd[:, :], func=AF.Sqrt, bias=eps_t[:, 0:1],
                                                   scale=1.0 / 1024), reads=['rstd', 'eps'], writes=['rstd'])
                S.op('dve', lambda e: e.reciprocal(out=rstd[:, :], in_=rstd[:, :]), reads=['rstd'], writes=['rstd'])
                for (b, R) in rowgroups:
                    slot = b - 1
                    S.op('dve', lambda e, R=R, slot=slot: e.scalar_tensor_tensor(
                        out=gvn[0:R, slot, :], in0=gvn[0:R, slot, :], scalar=rstd[0:R, slot:slot + 1],
                        in1=gain_b[0:R, :], op0=ALU.mult, op1=ALU.mult),
                        reads=[('gvn', slot), 'rstd', 'gain'], writes=[('gvn', slot)])
                    if b == NBLK - 1:
                        S.op('dve', lambda e, R=R, slot=slot: e.scalar_tensor_tensor(
                            out=ggf[0:R, :], in0=ggf[0:R, :], scalar=rstd[0:R, slot:slot + 1],
                            in1=gain_b[0:R, :], op0=ALU.mult, op1=ALU.mult),
                            reads=['ggf', 'rstd', 'gain'], writes=['ggf'])
                        S.dma('sp', gmv_o[l], ggf[:, :], reads=['ggf'])
                    if b == NBLK:
                        for j in range(4):
                            S.op('dve', lambda e, j=j, slot=slot: e.scalar_tensor_tensor(
                                out=tk[j][0:SG, :], in0=P[2 + j // 2][0:SG, (j % 2) * 256:(j % 2) * 256 + 256],
                                scalar=rstd[0:SG, slot:slot + 1], in1=gain_b[0:SG, j * 256:(j + 1) * 256],
                                op0=ALU.mult, op1=ALU.mult),
                                reads=[('P', 2 + j // 2), 'rstd', 'gain'], writes=[('tk', j)])
                            S.dma('sp', gmvs_o[l][:, j * 256:(j + 1) * 256], tk[j][0:SG, :], reads=[('tk', j)])

                def out_partial(Wsrc, row0, nk, actk, gtoff, g0):
                    tiles = col_tiles(g0, HC)
                    for half in range(2):
                        Wv, wk = load_slab(Wsrc[row0:row0 + nk * 128, half * 1024:(half + 1) * 1024], nk, 1024)
                        for mi in range(8):
                            m = half * 8 + mi
                            for ti, (a, bnd) in enumerate(tiles):
                                w = bnd - a
                                pi_ = (mi * len(tiles) + ti) % 4
                                pb, pk = P[pi_], ('P', pi_)
                                for k in range(nk):
                                    S.op('pe', lambda e, k=k, pb=pb, a=a, bnd=bnd, w=w, mi=mi: e.matmul(
                                        pb[:, 0:w], Wv[:, k, mi * 128:(mi + 1) * 128], actT[:, k, a - 128:bnd - 128],
                                        start=(k == 0), stop=(k == nk - 1)),
                                        reads=[wk, actk], writes=[pk], inc=(k == nk - 1))
                                wp = w - SG if bnd == HC else w
                                S.op('dve', lambda e, pb=pb, m=m, a=a, wp=wp: e.scalar_tensor_tensor(
                                    out=xT[:, m, a - 128:a - 128 + wp], in0=pb[:, 0:wp],
                                    scalar=cur['modL'][:, gtoff + m, 0:1], in1=xT[:, m, a - 128:a - 128 + wp],
                                    op0=ALU.mult, op1=ALU.add), reads=[pk, ('modT', l, 2), ('x', m)], writes=[('x', m)])
                                if bnd == HC:
                                    t3 = nst['t3']
                                    S.op('dve', lambda e, pb=pb, m=m, wp=wp: e.tensor_tensor(
                                        out=t3[:, 0:4], in0=pb[:, wp + 2:wp + 12:3], in1=cur['modL'][:, gtoff + m, 1:5],
                                        op=ALU.mult), reads=[pk, ('modT', l, 2)], writes=['t3'])
                                    S.op('dve', lambda e, m=m: e.tensor_tensor(
                                        out=xT[:, m, XS + 2:XS + 12:3], in0=t3[:, 0:4], in1=xT[:, m, XS + 2:XS + 12:3],
                                        op=ALU.add), reads=['t3', ('x', m)], writes=[('x', m)])

                for i in range(4):
                    Wv, wk = load_slab(w_in[l, :, 1536 + i * 256:1536 + (i + 1) * 256], 16, 256)
                    for j in range(2):
                        h = 2 * i + j
                        kk = h % 4
                        for ti, (a, bnd) in enumerate(col_tiles(fm0, HC)):
                            w = bnd - a
                            pb, pk = [(P[0], ('P', 0)), (P[1], ('P', 1)), (P[4], 'P4')][ti % 3]
                            for c in range(NCH):
                                S.op('pe', lambda e, c=c, pb=pb, a=a, bnd=bnd, w=w, j=j: e.matmul(
                                    pb[:, 0:w], Wv[:, c, j * 128:(j + 1) * 128], hT[:, c, a:bnd],
                                    start=(c == 0), stop=(c == NCH - 1)),
                                    reads=[wk, ('h', c)], writes=[pk], inc=(c == NCH - 1))
                            S.op('act', lambda e, pb=pb, a=a, bnd=bnd, w=w: e.activation(
                                out=actT[:, kk, a - 128:bnd - 128], in_=pb[:, 0:w], func=AF.Gelu_apprx_tanh),
                                reads=[pk], writes=['actT'])
                        grp = []
                        for (b, R) in rowgroups:
                            grp.append((b, R))
                            if len(grp) == 4 or b == NBLK:
                                pn = 2 + (b % 2)
                                pb, pk = P[pn], ('P', pn)
                                for gi_, (bb, RR) in enumerate(grp):
                                    rw = wmT[:, h, :] if bb < NBLK else wsd[:, h, :]
                                    rb = bsrow[0:1, h, :] if bb < NBLK else bs0[0:1, h, :]
                                    S.op('pe', lambda e, gi_=gi_, bb=bb, RR=RR, rw=rw, pb=pb: e.matmul(
                                        pb[:, gi_ * 128:gi_ * 128 + RR], gvn[0:RR, bb - 1, h * 128:(h + 1) * 128],
                                        rw[0:RR, 0:RR], start=True, stop=False),
                                        reads=[('gvn', bb - 1), 'wmT', 'wsd'], writes=[pk], inc=False)
                                    S.op('pe', lambda e, gi_=gi_, RR=RR, rb=rb, pb=pb: e.matmul(
                                        pb[:, gi_ * 128:gi_ * 128 + RR], ones_b[0:1, :], rb[0:1, 0:RR],
                                        start=False, stop=True),
                                        reads=['bsrow', 'bs0', 'ones_b'], writes=[pk], inc=(gi_ == len(grp) - 1))
                                b0_, R0 = grp[0]
                                xa = (b0_ - 1) * 128
                                for gi_, (bb, RR) in enumerate(grp):
                                    xa2 = (bb - 1) * 128 if bb < NBLK else XS
                                    S.op('dve', lambda e, gi_=gi_, RR=RR, xa2=xa2, pb=pb, kk=kk: e.tensor_tensor(
                                        out=actT[:, kk, xa2:xa2 + RR], in0=pb[:, gi_ * 128:gi_ * 128 + RR],
                                        in1=actT[:, kk, xa2:xa2 + RR], op=ALU.mult),
                                        reads=[pk, 'actT'], writes=['actT'])
                                grp = []
                    if i % 2 == 1:
                        out_partial(w_out[l], 1024 + (i // 2) * 512, 4, 'actT', 32, fm0)

                S.barrier()
                pa.close()
                s_ada.close()
                pbs = contextlib.ExitStack()
                KT = sb("KT", [64, NBLK + 1, 4, 128], BF16, pbs)
                Vg = sb("Vg", [128, NBLK + 1, 4, 65], BF16, pbs)
                cKT = sb("cKT", [64, 4, 128], BF16, pbs)
                cVg = sb("cVg", [128, 4, 65], BF16, pbs)
                tkc = sb("tkc", [128, 256], BF16, pbs)
                S.op('pool', lambda e: e.memset(Vg[:], 1.0), writes=['Vg'])
                S.op('pool', lambda e: e.memset(cVg[:], 1.0), writes=['cVg'])
                kvgroups = [(b, 128) for b in range(kv0 // 128, NBLK)] + [(NBLK, SG)]

                def rope(pb, pk, R, b, dst, dk):
                    src = pb[0:R, 0:256].rearrange("p (h d) -> p h d", h=4)
                    x1, x2 = src[:, :, 0:32], src[:, :, 32:64]
                    cb = ropec[0:R, b:b + 1, :].to_broadcast([R, 4, 32])
                    sn = ropes[0:R, b:b + 1, :].to_broadcast([R, 4, 32])
                    d3 = dst.rearrange("p (h d) -> p h d", h=4)
                    ta = tk[2][0:R, 0:128].rearrange("p (h d) -> p h d", h=4)
                    tb = tk[2][0:R, 128:256].rearrange("p (h d) -> p h d", h=4)
                    S.op('dve', lambda e: e.tensor_tensor(out=ta, in0=x1, in1=cb, op=ALU.mult),
                         reads=[pk, 'ropec'], writes=[('tk', 2)])
                    S.op('dve', lambda e: e.tensor_tensor(out=tb, in0=x2, in1=sn, op=ALU.mult),
                         reads=[pk, 'ropes'], writes=[('tk', 2)])
                    S.op('dve', lambda e: e.tensor_tensor(out=d3[:, :, 0:32], in0=ta, in1=tb, op=ALU.subtract),
                         reads=[('tk', 2)], writes=[dk])
                    S.op('dve', lambda e: e.tensor_tensor(out=ta, in0=x2, in1=cb, op=ALU.mult),
                         reads=[pk, 'ropec', dk], writes=[('tk', 2)])
                    S.op('dve', lambda e: e.tensor_tensor(out=tb, in0=x1, in1=sn, op=ALU.mult),
                         reads=[pk, 'ropes'], writes=[('tk', 2)])
                    S.op('dve', lambda e: e.tensor_tensor(out=d3[:, :, 32:64], in0=ta, in1=tb, op=ALU.add),
                         reads=[('tk', 2)], writes=[dk])

                def tr_heads(srcb, R, dstKT, dkey, skey='tkb'):
                    for hh in range(4):
                        S.op('pe', lambda e, hh=hh: e.transpose(out=PTb[0:64, hh * 128:hh * 128 + R],
                                                                in_=srcb[0:R, hh * 64:(hh + 1) * 64],
                                                                identity=ident_b[0:R, 0:R]),
                             reads=[skey, 'ident_b'], writes=['PTb'], inc=(hh == 3))
                    S.op('act', lambda e: e.activation(
                        out=dstKT, in_=PTb[0:64, 0:512].rearrange("p (h n) -> p h n", h=4)[:, :, 0:R], func=AF.Copy),
                        reads=['PTb'], writes=[dkey])

                Wk_, wkk = load_slab(w_in[l, :, 1024:1280], 16, 256)
                tkbs = [tkb, sb("tkb2", [128, 256], BF16, pbs)]
                tkbk = ['tkb', 'tkb2']

                def kstage1(n):
                    b, R = kvgroups[n]
                    pb, pk = P[n % 2], ('P', n % 2)
                    tokmm(Wk_, wkk, b, R, pb, pk)
                    kd = tk[n % 2]
                    rope(pb, pk, R, b, kd[0:R, :], ('tk', n % 2))
                    if b == NBLK - 1:
                        S.dma('sp', kwin_o[l], kd[:, :], reads=[('tk', n % 2)])
                    if b == NBLK:
                        for k in range(4):
                            S.dma('sp', kws_o[l, k, 127:128, :], kd[3 * k + 2:3 * k + 3, :], reads=[('tk', n % 2)])
                    S.op('pool', lambda e: e.tensor_copy(out=tkbs[n % 2][0:R, :], in_=kd[0:R, :]),
                         reads=[('tk', n % 2)], writes=[tkbk[n % 2]])

                def kstage2(n):
                    b, R = kvgroups[n]
                    tr_heads(tkbs[n % 2], R, KT[:, b, :, 0:R], 'KT', tkbk[n % 2])

                for t in range(len(kvgroups) + 1):
                    if t < len(kvgroups):
                        kstage1(t)
                    if t >= 1:
                        kstage2(t - 1)
                Wv_, wkv = load_slab(w_in[l, :, 1280:1536], 16, 256)
                for n, (b, R) in enumerate(kvgroups):
                    pb, pk = P[n % 2], ('P', n % 2)
                    tokmm(Wv_, wkv, b, R, pb, pk)
                    vd = tk[n % 2]
                    S.op('act', lambda e, vd=vd, pb=pb, R=R: e.activation(out=vd[0:R, :], in_=pb[0:R, 0:256],
                                                                          func=AF.Copy),
                         reads=[pk], writes=[('tk', n % 2)])
                    if b == NBLK - 1:
                        S.dma('sp', vwin_o[l], vd[:, :], reads=[('tk', n % 2)])
                    if b == NBLK:
                        for k in range(4):
                            S.dma('sp', vws_o[l, k, 127:128, :], vd[3 * k + 2:3 * k + 3, :], reads=[('tk', n % 2)])
                    S.op('dve', lambda e, vd=vd, R=R, b=b: e.tensor_copy(
                        out=Vg[0:R, b, :, 0:64], in_=vd[0:R, :].rearrange("p (h d) -> p h d", h=4)),
                        reads=[('tk', n % 2)], writes=['Vg'])
                for k in range(4):
                    S.dma('act', kws_o[l, k, 0:127, :], cache_k[l, k, 1:128, :])
                    S.dma('act', vws_o[l, k, 0:127, :], cache_v[l, k, 1:128, :])

                qTs = [sb("qT", [64, 4, 128], BF16, pbs), sb("qTb", [64, 4, 128], BF16, pbs)]
                PTss = [sb("PTs", [128, 2, 512], BF16, pbs), sb("PTsb", [128, 2, 512], BF16, pbs)]
                den = sb("den", [128, 8], F32, pbs)
                atok = sb("atok", [128, 256], BF16, pbs)
                sacc = tk[2][0:SG, :]

                def att_scores(nq, KTp, KTc, ncur, mP, mC, qb, qk, PTb_, ptk):
                    nn = 4 * nq
                    rq = qb[:, :, 0:nq]
                    S.op('pe', lambda e: e.matmul(PS1[:, 0:nn], KTp, rq, start=True, stop=(mP is None)),
                         reads=[qk, 'KT', 'cKT'], writes=['PS1'], inc=(mP is None))
                    if mP is not None:
                        for g in range(4):
                            S.op('pe', lambda e, g=g: e.matmul(PS1[:, g * 128:(g + 1) * 128], ident_b[:, :], mask_b[:, mP, :],
                                                               start=False, stop=(g == 3)),
                                 reads=['mask', 'ident_b'], writes=['PS1'], inc=(g == 3))
                    S.op('pe', lambda e: e.matmul(PS2[0:ncur, 0:nn], KTc, rq, start=True, stop=False),
                         reads=[qk, 'KT'], writes=['PS2'], inc=False)
                    if mC is None:
                        S.op('pe', lambda e: e.matmul(PS2[0:ncur, 0:nn], ident_b[0:ncur, 0:ncur], maskD_b[0:ncur, 0:nn],
                                                      start=False, stop=True),
                             reads=['mask', 'ident_b'], writes=['PS2'])
                    else:
                        for g in range(4):
                            S.op('pe', lambda e, g=g: e.matmul(PS2[:, g * 128:(g + 1) * 128], ident_b[:, :], mask_b[:, mC, :],
                                                               start=False, stop=(g == 3)),
                                 reads=['mask', 'ident_b'], writes=['PS2'], inc=(g == 3))
                    S.op('act', lambda e: e.activation(out=PTb_[:, 0, 0:nn], in_=PS1[:, 0:nn], func=AF.Exp, scale=0.125),
                         reads=['PS1'], writes=[ptk])
                    S.op('act', lambda e: e.activation(out=PTb_[0:ncur, 1, 0:nn], in_=PS2[0:ncur, 0:nn], func=AF.Exp,
                                                       scale=0.125), reads=['PS2'], writes=[ptk])

                def att_pv(kvh, nq, Vp, Vc, ncur, PTb_, ptk):
                    O = P[4][0:nq, 0:260].rearrange("p (g d) -> p g d", g=4)
                    for g in range(4):
                        S.op('pe', lambda e, g=g: e.matmul(O[:, g, :], PTb_[:, 0, g * nq:(g + 1) * nq], Vp,
                                                           start=True, stop=False),
                             reads=[ptk, 'Vg', 'cVg'], writes=['P4'], inc=False)
                        S.op('pe', lambda e, g=g: e.matmul(O[:, g, :], PTb_[0:ncur, 1, g * nq:(g + 1) * nq], Vc,
                                                           start=False, stop=True),
                             reads=[ptk, 'Vg'], writes=['P4'], inc=(g == 3))
                    S.op('dve', lambda e: e.tensor_tensor(out=den[0:nq, 0:4], in0=O[:, :, 64],
                                                          in1=esink[0:nq, kvh * 4:kvh * 4 + 4], op=ALU.add),
                         reads=['P4', 'esink'], writes=['den'])
                    S.op('dve', lambda e: e.reciprocal(out=den[0:nq, 4:8], in_=den[0:nq, 0:4]),
                         reads=['den'], writes=['den'])
                    return O

                def stageA1(i, n, Wv, wk):
                    b, R = rowgroups[n]
                    pb, pk = P[n % 2], ('P', n % 2)
                    tokmm(Wv, wk, b, R, pb, pk)
                    qd = tk[n % 2]
                    rope(pb, pk, R, b, qd[0:R, :], ('tk', n % 2))
                    S.op('pool', lambda e: e.tensor_copy(out=tkbs[n % 2][0:R, :], in_=qd[0:R, :]),
                         reads=[('tk', n % 2)], writes=[tkbk[n % 2]])

                def stageA2(i, n):
                    b, R = rowgroups[n]
                    tr_heads(tkbs[n % 2], R, qTs[n % 2][:, :, 0:R], ('qT', n % 2), tkbk[n % 2])

                def stageB(i, n):
                    b, R = rowgroups[n]
                    if b < NBLK:
                        att_scores(128, KT[:, b - 1, i, :], KT[:, b, i, :], 128, 2 if b == 3 else 0, 1,
                                   qTs[n % 2], ('qT', n % 2), PTss[n % 2], ('PTs', n % 2))

                def stageC(i, n):
                    b, R = rowgroups[n]
                    PTb_, ptk = PTss[n % 2], ('PTs', n % 2)
                    if b < NBLK:
                        O = att_pv(i, 128, Vg[:, b - 1, i, :], Vg[:, b, i, :], 128, PTb_, ptk)
                        S.op('dve', lambda e: e.tensor_tensor(
                            out=atok[:, :].rearrange("p (g d) -> p g d", g=4), in0=O[:, :, 0:64],
                            in1=den[:, 4:8].unsqueeze(2).to_broadcast([128, 4, 64]), op=ALU.mult),
                            reads=['P4', 'den'], writes=['atok'])
                    else:
                        S.op('dve', lambda e: e.memset(sacc[:], 0.0), writes=[('tk', 2)])
                        for k in range(4):
                            att_scores(SG, cKT[:, k, :], KT[:, NBLK, i, 0:SG], SG, None, None,
                                       qTs[n % 2], ('qT', n % 2), PTb_, ptk)
                            O = att_pv(i, SG, cVg[:, k, :], Vg[0:SG, NBLK, i, :], SG, PTb_, ptk)
                            S.op('dve', lambda e: e.tensor_tensor(
                                out=tk[3][0:SG, :].rearrange("p (g d) -> p g d", g=4), in0=O[:, :, 0:64],
                                in1=den[0:SG, 4:8].unsqueeze(2).to_broadcast([SG, 4, 64]), op=ALU.mult),
                                reads=['P4', 'den'], writes=[('tk', 3)])
                            S.op('dve', lambda e, k=k: e.scalar_tensor_tensor(
                                out=sacc[:, :], in0=tk[3][0:SG, :], scalar=ident_f[0:SG, 3 * k + 2:3 * k + 3],
                                in1=sacc[:, :], op0=ALU.mult, op1=ALU.add),
                                reads=[('tk', 3), 'ident_f', ('tk', 2)], writes=[('tk', 2)])
                        S.op('dve', lambda e: e.tensor_copy(out=atok[0:SG, :], in_=sacc[:, :]),
                             reads=[('tk', 2)], writes=['atok'])
                    xa = (b - 1) * 128 if b < NBLK else XS
                    for j in range(2):
                        S.op('pe', lambda e, j=j: e.transpose(out=PTb[:, 512 + j * 128:512 + j * 128 + R],
                                                              in_=atok[0:R, j * 128:(j + 1) * 128],
                                                              identity=ident_b[0:R, 0:R]),
                             reads=['atok', 'ident_b'], writes=['PTb'], inc=(j == 1))
                    S.op('act', lambda e: e.activation(
                        out=actT[:, 2 * (i % 2):2 * (i % 2) + 2, xa:xa + R],
                        in_=PTb[:, 512:768].rearrange("p (j n) -> p j n", j=2)[:, :, 0:R], func=AF.Copy),
                        reads=['PTb'], writes=['actT'])

                nj = len(rowgroups)
                for i in range(4):
                    Wv, wk = load_slab(w_in[l, :, i * 256:(i + 1) * 256], 16, 256)
                    for k in range(4):
                        S.dma('pool', tkc[:, k * 64:(k + 1) * 64], cache_k[l, k][:, i * 64:(i + 1) * 64], writes=['tkc'])
                        S.dma('pool', cVg[:, k, 0:64], cache_v[l, k][:, i * 64:(i + 1) * 64], writes=['cVg'])
                    tr_heads(tkc, 128, cKT[:, :, :], 'cKT', 'tkc')
                    for t in range(nj + 3):
                        if t < nj:
                            stageA1(i, t, Wv, wk)
                        if 0 <= t - 1 < nj:
                            stageA2(i, t - 1)
                        if 0 <= t - 2 < nj:
                            stageB(i, t - 2)
                        if 0 <= t - 3 < nj:
                            stageC(i, t - 3)
                    if i % 2 == 1:
                        out_partial(w_out[l], (i - 1) * 256, 4, 'actT', 32, fm0)
                S.barrier()
                pbs.close()

            with contextlib.ExitStack() as fx:
                t3_ = sb("t3", [128, 4], F32, fx)
                nst2 = contextlib.ExitStack()
                nst = {'sq': [sb("sq%d" % i, [128, 336], BF16, nst2) for i in range(4)],
                       'rs': sb("rs", [128, 336], F32, nst2),
                       't2': [sb("t2%d" % i, [128, 336], F32, nst2) for i in range(2)],
                       't3': t3_}
                compute_A(1)
                norm(1, 1, n0, HC, nst)
                S.barrier()
                nst2.close()
                W_ = HC - n0
                halo = 2
                aFs = [sb("aF%d" % i, [128, W_ + 2 - halo], F32, fx) for i in range(2)]
                c1 = sb("c1", [128, 512], F32, fx)
                c2 = sb("c2", [128, 512], F32, fx)
                gTt = [sb("gTt%d" % i, [128, 4, XC], BF16, fx) for i in range(2)]
                akeep = sb("akeep", [128, NFF, 10], F32, fx)
                stT = sb("stT", [128, NFF, 8], F32, fx)
                adap = None
                if l == 0:
                    Wx = sb("Wx", [128, 4096], BF16, fx)
                    adap = AdaPipe(1, [(Wx, 0, ('wx', 0)), (Wx, 2048, ('wx', 1))])
                for fc in range(NFF):
                    if fc % 4 == 0:
                        S.dma('sp', c2[0:8, 0:512], state[l, :, fc * 128:fc * 128 + 512], writes=['c2'])
                    S.op('pe', lambda e, fc=fc: e.transpose(out=P[4][:, (fc % 4) * 8:(fc % 4) * 8 + 8],
                                                            in_=c2[0:8, (fc % 4) * 128:(fc % 4 + 1) * 128],
                                                            identity=ident_f[0:8, 0:8]),
                         reads=['c2', 'ident_f'], writes=['P4'], inc=(fc % 4 == 3))
                    if fc % 4 == 3:
                        S.op('dve', lambda e, fc=fc: e.tensor_copy(
                            out=stT[:, fc - 3:fc + 1, :], in_=P[4][:, 0:32].rearrange("p (a b) -> p a b", a=4)),
                            reads=['P4'], writes=['stT'])
                if l == 0:
                    for q2 in range(2):
                        S.op('dve', lambda e, q2=q2: e.memset(aFs[q2][:, 0:2], 0.0), writes=[('aF', q2)])
                tiles = col_tiles(n0, HC)
                so = GS - n0 + 2 - halo
                hidx = 382 - n0 + 2 - halo

                def gate_part(fc, Wg, wkg, aFb, afk):
                    jj = fc % 2
                    gbanks = [(P[0], ('P', 0)), (P[1], ('P', 1))] + ([(PS1, 'PS1')] if l == 1 else [(PTb[:, :].bitcast(F32), 'PTb')])
                    for ti, (a, bnd) in enumerate(tiles):
                        w = bnd - a
                        pb, pk = gbanks[ti % len(gbanks)]
                        for c in range(NCH):
                            S.op('pe', lambda e, c=c: e.matmul(
                                pb[:, 0:w], Wg[:, c, jj * 128:(jj + 1) * 128], hT[:, c, a:bnd],
                                start=(c == 0), stop=(c == NCH - 1)),
                                reads=[wkg, ('h', c)], writes=[pk], inc=(c == NCH - 1))
                        ao = a - n0 + 2 - halo
                        S.op('act', lambda e: e.activation(out=aFb[:, ao:ao + w], in_=pb[:, 0:w], func=AF.Copy),
                             reads=[pk], writes=[afk])
                    S.op('dve', lambda e: e.tensor_scalar(out=aFb[:, hidx:hidx + 2], in0=aFb[:, hidx:hidx + 2],
                                                          scalar1=flag_t[:, 0:1], scalar2=None, op0=ALU.mult),
                         reads=[afk, 'flag'], writes=[afk])
                    S.op('dve', lambda e: e.tensor_copy(
                        out=aFb[:, so:so + 12].rearrange("p (k t) -> p k t", k=4)[:, :, 0:2],
                        in_=stT[:, fc, :].rearrange("p (k t) -> p k t", k=4)),
                        reads=[afk, 'stT'], writes=[afk])
                    S.op('dve', lambda e: e.tensor_copy(out=akeep[:, fc, 0:2], in_=aFb[:, so - 2:so]),
                         reads=[afk], writes=['akeep'])
                    S.op('dve', lambda e: e.tensor_copy(
                        out=akeep[:, fc, 2:10].rearrange("p (k t) -> p k t", k=4),
                        in_=aFb[:, so:so + 12].rearrange("p (k t) -> p k t", k=4)[:, :, 1:3]),
                        reads=[afk], writes=['akeep'])

                def up_part(fc, Wu, wku, aFb, afk):
                    jj = fc % 2
                    j = fc % 4
                    gt_ = gTt[(fc // 4) % 2]
                    gk = ('gTt', (fc // 4) % 2)
                    for ti, (a, bnd) in enumerate(col_tiles(n0 + halo, HC)):
                        w = bnd - a
                        pb, pk = [(P[2], ('P', 2)), (P[3], ('P', 3)), (P[4], 'P4')][ti % 3]
                        for c in range(NCH):
                            S.op('pe', lambda e, c=c: e.matmul(
                                pb[:, 0:w], Wu[:, c, jj * 128:(jj + 1) * 128], hT[:, c, a:bnd],
                                start=(c == 0), stop=(c == NCH - 1)),
                                reads=[wku, ('h', c)], writes=[pk], inc=(c == NCH - 1))
                        ao = a - (n0 + halo)
                        S.op('act', lambda e: e.activation(
                            out=c1[:, 0:w], in_=aFb[:, ao:ao + w], func=AF.Identity,
                            bias=cbT[:, fc:fc + 1], scale=cwT[:, fc:fc + 1]),
                            reads=[afk, 'cwT', 'cbT'], writes=['c1'])
                        S.op('dve', lambda e: e.scalar_tensor_tensor(
                            out=c2[:, 0:w], in0=aFb[:, ao + 1:ao + 1 + w], scalar=cwT[:, 44 + fc:45 + fc],
                            in1=c1[:, 0:w], op0=ALU.mult, op1=ALU.add),
                            reads=[afk, 'cwT', 'c1'], writes=['c2'])
                        S.op('dve', lambda e: e.scalar_tensor_tensor(
                            out=c1[:, 0:w], in0=aFb[:, ao + 2:ao + 2 + w], scalar=cwT[:, 88 + fc:89 + fc],
                            in1=c2[:, 0:w], op0=ALU.mult, op1=ALU.add),
                            reads=[afk, 'cwT', 'c2'], writes=['c1'])
                        S.op('act', lambda e: e.activation(out=c2[:, 0:w], in_=c1[:, 0:w], func=AF.Silu),
                             reads=['c1'], writes=['c2'])
                        S.op('dve', lambda e: e.tensor_tensor(
                            out=gt_[:, j, a - 128:bnd - 128], in0=c2[:, 0:w], in1=pb[:, 0:w], op=ALU.mult),
                            reads=['c2', pk], writes=[gk])

                for fp in range(NFF // 2):
                    Wg, wkg = load_slab(w_gate[l, :, fp * 256:fp * 256 + 256], 16, 256)
                    Wu, wku = load_slab(w_up[l, :, fp * 256:fp * 256 + 256], 16, 256)
                    for q2 in range(2):
                        gate_part(2 * fp + q2, Wg, wkg, aFs[q2], ('aF', q2))
                    for q2 in range(2):
                        fc = 2 * fp + q2
                        up_part(fc, Wu, wku, aFs[q2], ('aF', q2))
                        if adap is not None:
                            want = (98 * (fc + 1)) // NFF
                            while adap.m < want and not adap.done():
                                adap.step()
                    fc = 2 * fp + 1
                    j = fc % 4
                    grp = fc // 4
                    gt_ = gTt[grp % 2]
                    gk = ('gTt', grp % 2)
                    if j == 3:
                        gtiles = col_tiles(n0 + halo, HC)
                        for half in range(2):
                            Wv, wk = load_slab(w_down[l, grp * 512:(grp + 1) * 512, half * 1024:(half + 1) * 1024], 4, 1024)
                            for mi in range(8):
                                m = half * 8 + mi
                                for ti, (a, bnd) in enumerate(gtiles):
                                    w = bnd - a
                                    pi_ = (mi * len(gtiles) + ti) % 4
                                    pb, pk = P[pi_], ('P', pi_)
                                    for k in range(4):
                                        S.op('pe', lambda e, k=k, pb=pb, a=a, bnd=bnd, w=w, mi=mi, Wv=Wv, gt_=gt_: e.matmul(
                                            pb[:, 0:w], Wv[:, k, mi * 128:(mi + 1) * 128], gt_[:, k, a - 128:bnd - 128],
                                            start=(k == 0), stop=(k == 3)),
                                            reads=[wk, gk], writes=[pk], inc=(k == 3))
                                    wp = w - SG if bnd == HC else w
                                    S.op('dve', lambda e, pb=pb, m=m, a=a, wp=wp: e.scalar_tensor_tensor(
                                        out=xT[:, m, a - 128:a - 128 + wp], in0=pb[:, 0:wp],
                                        scalar=cur['modL'][:, 80 + m, 0:1], in1=xT[:, m, a - 128:a - 128 + wp],
                                        op0=ALU.mult, op1=ALU.add), reads=[pk, ('modT', l, 5), ('x', m)], writes=[('x', m)])
                                    if bnd == HC:
                                        t3 = nst['t3']
                                        S.op('dve', lambda e, pb=pb, m=m, wp=wp: e.tensor_tensor(
                                            out=t3[:, 0:4], in0=pb[:, wp + 2:wp + 12:3], in1=cur['modL'][:, 80 + m, 1:5],
                                            op=ALU.mult), reads=[pk, ('modT', l, 5)], writes=['t3'])
                                        S.op('dve', lambda e, m=m: e.tensor_tensor(
                                            out=xT[:, m, XS + 2:XS + 12:3], in0=t3[:, 0:4],
                                            in1=xT[:, m, XS + 2:XS + 12:3], op=ALU.add),
                                            reads=['t3', ('x', m)], writes=[('x', m)])
                for q4 in range(11):
                    for jx in range(4):
                        fc = q4 * 4 + jx
                        S.op('pe', lambda e, fc=fc, jx=jx: e.transpose(out=P[4][0:10, jx * 128:(jx + 1) * 128],
                                                                       in_=akeep[:, fc, :], identity=ident_f[:, :]),
                             reads=['akeep', 'ident_f'], writes=['P4'], inc=(jx == 3))
                    S.op('act', lambda e: e.activation(out=c1[0:10, 0:512], in_=P[4][0:10, 0:512], func=AF.Copy),
                         reads=['P4'], writes=['c1'])
                    S.dma('sp', akeep_o[l, :, q4 * 512:(q4 + 1) * 512], c1[0:10, 0:512], reads=['c1'])
                while adap is not None and not adap.done():
                    adap.step()
                S.barrier()

        with contextlib.ExitStack() as fo:
            sq = [sb("fsq%d" % i, [128, 256], BF16, fo) for i in range(2)]
            rs = sb("frs", [128, 256], F32, fo)
            yt = [sb("yt%d" % i, [128, 128], F32, fo) for i in range(2)]
            yo = [sb("yo%d" % i, [128, D], F32, fo) for i in range(2)]
            groups = [(b, 128) for b in range(3, NBLK)] + [(NBLK, SG)]
            for n, (b, R) in enumerate(groups):
                xa = (b - 1) * 128 if b < NBLK else XS
                for c in range(NCH):
                    S.op('act', lambda e, c=c, xa=xa, R=R: e.activation(out=sq[c % 2][:, 0:R], in_=xT[:, c, xa:xa + R],
                                                                        func=AF.Square),
                         reads=[('x', c)], writes=[('fsq', c % 2)])
                    S.op('pe', lambda e, c=c, R=R: e.matmul(P[4][:, 0:R], ones_b[:], sq[c % 2][:, 0:R],
                                                            start=(c == 0), stop=(c == NCH - 1)),
                         reads=[('fsq', c % 2), 'ones_b'], writes=['P4'])
                S.op('act', lambda e, R=R: e.activation(out=rs[:, 0:R], in_=P[4][:, 0:R], func=AF.Sqrt,
                                                        bias=eps_t[:, 0:1], scale=1.0 / D),
                     reads=['P4', 'eps'], writes=['frs'])
                S.op('dve', lambda e, R=R: e.reciprocal(out=rs[:, 0:R], in_=rs[:, 0:R]), reads=['frs'], writes=['frs'])
                yob = yo[n % 2]
                for c in range(NCH):
                    ytt = yt[c % 2]
                    S.op('dve', lambda e, c=c, xa=xa, R=R, ytt=ytt: e.scalar_tensor_tensor(
                        out=ytt[:, 0:R], in0=xT[:, c, xa:xa + R], scalar=gT[:, 2, c:c + 1], in1=rs[:, 0:R],
                        op0=ALU.mult, op1=ALU.mult), reads=[('x', c), 'gT2', 'frs'], writes=[('yt', c % 2)])
                    pn = c % 4
                    S.op('pe', lambda e, R=R, ytt=ytt, pn=pn: e.transpose(out=P[pn][0:R, 0:128], in_=ytt[:, 0:R],
                                                                          identity=ident_f[:, :]),
                         reads=[('yt', c % 2), 'ident_f'], writes=[('P', pn)])
                    S.op('act', lambda e, c=c, R=R, pn=pn, yob=yob: e.activation(
                        out=yob[0:R, c * 128:(c + 1) * 128], in_=P[pn][0:R, 0:128], func=AF.Copy),
                        reads=[('P', pn)], writes=[('yo', n % 2)])
                if b < NBLK:
                    S.dma('sp', y_o[b - 3], yob[:, :], reads=[('yo', n % 2)])
                else:
                    S.dma('sp', ys_o, yob[0:SG, :], reads=[('yo', n % 2)])
        S.finish()
    return nc


_NC_CACHE = {}


def _rope_tables(pos):
    inv = (np.float32(10000.0) ** (-(np.arange(32, dtype=np.float32) / np.float32(32)))).astype(np.float32)
    ang = (pos.astype(np.float32)[:, None] * inv[None, :]).astype(np.float32)
    return np.cos(ang.astype(np.float64)).astype(np.float32), np.sin(ang.astype(np.float64)).astype(np.float32)


def kernel(**inputs):
    f = lambda k: np.ascontiguousarray(np.asarray(inputs[k], dtype=np.float32))
    x_prompt = f("x_prompt"); x_sample = f("x_sample")
    cache_k = f("cache_k"); cache_v = f("cache_v"); state_conv = f("state_conv")
    c_prompt = f("c_prompt"); c_sample = f("c_sample")
    shared = {
        "w_ada": f("w_ada"), "b_ada": f("b_ada").reshape(2, 96, 128),
        "g1": f("g_norm1").reshape(2, 16, 128), "g2": f("g_norm2").reshape(2, 16, 128),
        "gf": f("g_final").reshape(16, 128), "w_in": f("w_in"), "gm_gain": f("gm_gain"),
        "gm_ws": f("gm_ws"), "gm_bs": f("gm_bs").reshape(2, 1024), "sinks": f("sinks"),
        "w_out": f("w_out"), "w_gate": f("w_gate"), "w_up": f("w_up"),
        "conv_w": f("conv_w").reshape(2, 132, 128), "conv_b": f("conv_b").reshape(2, 44, 128),
        "w_down": f("w_down"), "ident": np.eye(128, dtype=np.float32),
    }
    jj = np.arange(128)[:, None]; ii = np.arange(128)[None, :]
    mP = np.where(jj >= ii, 0.0, NEG).astype(np.float32)
    mC = np.where(jj <= ii, 0.0, NEG).astype(np.float32)
    mD = np.full((128, 48), NEG, np.float32)
    for g in range(4):
        for i in range(SG):
            mD[i, g * SG + i] = 0.0
    in_maps = []
    for ci in range(8):
        bi, half = ci // 2, ci % 2
        s0 = half * 1024
        xb = np.zeros((NBLK, 128, D), np.float32)
        pos = np.zeros((NBLK + 1, 128), np.int64)
        for b in range(NBLK):
            t0 = s0 - 384 + 128 * b
            pos[b] = np.maximum(t0 + np.arange(128), 0)
            if t0 >= 0:
                xb[b] = x_prompt[bi, t0:t0 + 128]
        pos[NBLK] = PAST
        cs, sn = _rope_tables(pos.reshape(-1))
        xs12 = np.zeros((SG, D), np.float32)
        for k in range(4):
            xs12[3 * k + 2] = x_sample[4 * ci + k, 0]
        msk = np.stack([mP, mC, mP if half == 1 else np.full((128, 128), NEG, np.float32)])
        m = dict(shared)
        m.update({
            "xblk": xb, "xs": xs12,
            "cvec": np.concatenate([c_prompt[bi:bi + 1], c_sample[4 * ci:4 * ci + 4]], axis=0),
            "cache_k": np.ascontiguousarray(cache_k[:, 4 * ci:4 * ci + 4].reshape(2, 4, 128, 256)),
            "cache_v": np.ascontiguousarray(cache_v[:, 4 * ci:4 * ci + 4].reshape(2, 4, 128, 256)),
            "state": np.ascontiguousarray(state_conv[:, 4 * ci:4 * ci + 4].reshape(2, 8, DFF)),
            "rope_c": cs.reshape(NBLK + 1, 128, 32), "rope_s": sn.reshape(NBLK + 1, 128, 32),
            "masks": msk.astype(np.float32), "maskd": mD, "flag": np.full((128, 1), float(half), np.float32),
        })
        in_maps.append(m)
    if "nc" not in _NC_CACHE:
        _NC_CACHE["nc"] = build()
    res = run_bass_kernel_spmd(_NC_CACHE["nc"], in_maps, core_ids=list(range(8)))
    R = res.results
    y_prompt = np.zeros((4, 2048, D), np.float32)
    y_sample = np.zeros((32, 1, D), np.float32)
    kwp = np.zeros((2, 4, 128, 4, 64), np.float32); vwp = np.zeros_like(kwp)
    gmp = np.zeros((2, 4, 128, 1024), np.float32)
    cvp = np.zeros((2, 4, 2, DFF), np.float32)
    kws = np.zeros((2, 32, 128, 4, 64), np.float32); vws = np.zeros_like(kws)
    gms = np.zeros((2, 32, 1, 1024), np.float32)
    cvs = np.zeros((2, 32, 2, DFF), np.float32)
    for ci in range(8):
        bi, half = ci // 2, ci % 2
        r = R[ci]
        y_prompt[bi, half * 1024:(half + 1) * 1024] = np.asarray(r["y"]).reshape(1024, D)
        ys = np.asarray(r["ys"]); ak = np.asarray(r["akeep"]); gv = np.asarray(r["gmvs"])
        for k in range(4):
            y_sample[4 * ci + k, 0] = ys[3 * k + 2]
            gms[:, 4 * ci + k, 0] = gv[:, 3 * k + 2]
            cvs[:, 4 * ci + k] = ak[:, 2 + 2 * k:4 + 2 * k]
        kws[:, 4 * ci:4 * ci + 4] = np.asarray(r["kws"]).reshape(2, 4, 128, 4, 64)
        vws[:, 4 * ci:4 * ci + 4] = np.asarray(r["vws"]).reshape(2, 4, 128, 4, 64)
        if half == 1:
            kwp[:, bi] = np.asarray(r["kwin"]).reshape(2, 128, 4, 64)
            vwp[:, bi] = np.asarray(r["vwin"]).reshape(2, 128, 4, 64)
            gmp[:, bi] = np.asarray(r["gmv"])
            cvp[:, bi] = ak[:, 0:2]
    return (y_prompt, y_sample, kwp, vwp, gmp, cvp, kws, vws, gms, cvs)
```
